# Optimizing a Trainium2 kernel written in Bass

```python
import math
import jax, jax.numpy as jnp
from jax import lax
import numpy as np

D_MODEL = 2048
BATCH = 2
SEQ = 8192
DEPTH = 2

A_HEADS = 8
A_HEAD_DIM = 128
DILATED_GROUPS = ((128, 1), (512, 4), (2048, 16))
N_DIL = len(DILATED_GROUPS)
ATT_BLOCK = 128
ROPE_THETA = 500000.0
ROT_DIM = A_HEAD_DIM // 4
A_WIDTH = A_HEADS * A_HEAD_DIM
A_QKV = N_DIL * A_WIDTH
RET_HEADS = 8
RET_QK_DIM = 128
RET_V_DIM = 2 * RET_QK_DIM
RET_CHUNK = 128
RET_ROPE_THETA = 10000.0
RET_QK_WIDTH = RET_HEADS * RET_QK_DIM
RET_V_WIDTH = RET_HEADS * RET_V_DIM
EVEN_WIDTH = A_WIDTH + RET_V_WIDTH
EVEN_SPLITS = (A_QKV, A_QKV, A_QKV, RET_QK_WIDTH, RET_QK_WIDTH, RET_V_WIDTH, EVEN_WIDTH)
EVEN_IN = sum(EVEN_SPLITS)
CONV_WIDTH = 2 * D_MODEL
CONV_KERNEL = 31
EPS = 1e-6
NEG_INF = -1e30
N_EVEN = (DEPTH + 1) // 2
N_ODD = DEPTH // 2

kernel_name = "hybrid_dilated_retention_conformer"


def rms_norm(x, g):
    xf = x.astype(jnp.float32)
    y = xf * lax.rsqrt(jnp.mean(xf * xf, axis=-1, keepdims=True) + EPS)
    return (y * g.astype(jnp.float32)).astype(x.dtype)


def layer_norm(x, g, b):
    xf = x.astype(jnp.float32)
    mu = jnp.mean(xf, axis=-1, keepdims=True)
    var = jnp.mean(jnp.square(xf - mu), axis=-1, keepdims=True)
    y = (xf - mu) * lax.rsqrt(var + EPS)
    return (y * g.astype(jnp.float32) + b.astype(jnp.float32)).astype(x.dtype)


def rope(x, pos, rot_dim, theta):
    half = rot_dim // 2
    inv_freq = theta ** (-jnp.arange(0, rot_dim, 2, dtype=jnp.float32) / rot_dim)
    ang = pos[:, None] * inv_freq[None, :]
    shape = (ang.shape[0],) + (1,) * (x.ndim - 3) + (half,)
    cos = jnp.cos(ang).reshape(shape)
    sin = jnp.sin(ang).reshape(shape)
    xf = x.astype(jnp.float32)
    x1, x2, rest = xf[..., :half], xf[..., half:rot_dim], xf[..., rot_dim:]
    out = jnp.concatenate([x1 * cos - x2 * sin, x2 * cos + x1 * sin, rest], axis=-1)
    return out.astype(x.dtype)


def dilated_group_attention(q, k, v, window, dilation):
    B, S, H, dh = q.shape
    L = S // dilation
    nb = -(-L // ATT_BLOCK)
    Lp = nb * ATT_BLOCK
    w_sub = window // dilation

    def to_blocks(t):
        t = t.reshape(B, L, dilation, H, dh).transpose(0, 2, 3, 1, 4)
        t = jnp.pad(t, ((0, 0), (0, 0), (0, 0), (0, Lp - L), (0, 0)))
        return t.reshape(B, dilation, H, nb, ATT_BLOCK, dh)

    def with_prev(t):
        prev = jnp.concatenate([jnp.zeros_like(t[:, :, :, :1]), t[:, :, :, :-1]], axis=3)
        return jnp.concatenate([prev, t], axis=4)

    qb = to_blocks(q)
    kk = with_prev(to_blocks(k))
    vv = with_prev(to_blocks(v))
    s = jnp.einsum('bghnqc,bghnkc->bghnqk', qb, kk).astype(jnp.float32) * (dh ** -0.5)
    i = jnp.arange(ATT_BLOCK)[:, None]
    j = jnp.arange(2 * ATT_BLOCK)[None, :]
    dist = i + ATT_BLOCK - j
    band = (dist >= 0) & (dist <= w_sub)
    valid_prev = (jnp.arange(nb)[:, None, None] > 0) | (j >= ATT_BLOCK)[None]
    mask = band[None] & valid_prev
    s = jnp.where(mask, s, NEG_INF)
    lse = jax.nn.logsumexp(s, axis=-1)
    p = jnp.exp(s - lse[..., None])
    o = jnp.einsum('bghnqk,bghnkc->bghnqc', p.astype(v.dtype), vv)
    o = o.reshape(B, dilation, H, Lp, dh)[:, :, :, :L]
    o = o.transpose(0, 3, 1, 2, 4).reshape(B, S, H, dh)
    lse = lse.reshape(B, dilation, H, Lp)[:, :, :, :L]
    lse = lse.transpose(0, 3, 1, 2).reshape(B, S, H)
    return o, lse


def retention(q, k, v):
    B, S, H, dk = q.shape
    dv = v.shape[-1]
    C = RET_CHUNK
    N = S // C
    dt = q.dtype

    def to_chunks(t):
        return t.reshape(B, N, C, H, t.shape[-1]).transpose(0, 3, 1, 2, 4)

    qc, kc, vc = to_chunks(q), to_chunks(k), to_chunks(v)
    log_g = jnp.log1p(-jnp.power(2.0, -5.0 - jnp.arange(H, dtype=jnp.float32)))
    idx = jnp.arange(C, dtype=jnp.float32)
    diff = idx[:, None] - idx[None, :]
    dmask = jnp.where(diff >= 0, jnp.exp(jnp.maximum(diff, 0.0)[None] * log_g[:, None, None]), 0.0)
    scores = jnp.einsum('bhnid,bhnjd->bhnij', qc, kc) * dmask[None, :, None].astype(dt)
    intra = jnp.einsum('bhnij,bhnje->bhnie', scores, vc)
    k_dec = kc * jnp.exp((C - 1 - idx)[None, :] * log_g[:, None])[None, :, None, :, None].astype(dt)
    kv = jnp.einsum('bhnjd,bhnje->nbhde', k_dec, vc)
    chunk_decay = jnp.exp(C * log_g)[None, :, None, None].astype(dt)

    def step(state, kv_n):
        return state * chunk_decay + kv_n, state

    _, s_prev = lax.scan(step, jnp.zeros((B, H, dk, dv), dt), kv)
    q_dec = qc * jnp.exp((idx + 1.0)[None, :] * log_g[:, None])[None, :, None, :, None].astype(dt)
    cross = jnp.einsum('bhnid,nbhde->bhnie', q_dec, s_prev)
    out = (intra + cross).transpose(0, 2, 3, 1, 4).reshape(B, S, H, dv)
    return out


def head_rms(x):
    xf = x.astype(jnp.float32)
    return (xf * lax.rsqrt(jnp.mean(xf * xf, axis=-1, keepdims=True) + EPS)).astype(x.dtype)


def even_layer(x, norm_g, w_in, w_out, pos):
    B, S, _ = x.shape
    h = rms_norm(x, norm_g)
    proj = h @ w_in
    cuts = list(np.cumsum(EVEN_SPLITS)[:-1])
    aq, ak, av, bq, bk, bv, gate = jnp.split(proj, cuts, axis=-1)
    aq = rope(aq.reshape(B, S, N_DIL, A_HEADS, A_HEAD_DIM), pos, ROT_DIM, ROPE_THETA)
    ak = rope(ak.reshape(B, S, N_DIL, A_HEADS, A_HEAD_DIM), pos, ROT_DIM, ROPE_THETA)
    av = av.reshape(B, S, N_DIL, A_HEADS, A_HEAD_DIM)
    outs, lses = [], []
    for g, (window, dilation) in enumerate(DILATED_GROUPS):
        o, l = dilated_group_attention(aq[:, :, g], ak[:, :, g], av[:, :, g], window, dilation)
        outs.append(o)
        lses.append(l)
    wts = jax.nn.softmax(jnp.stack(lses, axis=0), axis=0)
    a_out = jnp.sum(wts[..., None].astype(x.dtype) * jnp.stack(outs, axis=0), axis=0)
    bq = rope(bq.reshape(B, S, RET_HEADS, RET_QK_DIM), pos, RET_QK_DIM, RET_ROPE_THETA)
    bk = rope(bk.reshape(B, S, RET_HEADS, RET_QK_DIM), pos, RET_QK_DIM, RET_ROPE_THETA) * (RET_QK_DIM ** -0.5)
    bv = bv.reshape(B, S, RET_HEADS, RET_V_DIM)
    b_out = head_rms(retention(bq, bk, bv))
    y = jnp.concatenate([a_out.reshape(B, S, A_WIDTH), b_out.reshape(B, S, RET_V_WIDTH)], axis=-1)
    y = y * jax.nn.silu(gate)
    return x + y @ w_out


def odd_layer(x, norm_g, w_in, b_in, conv_w, conv_b, ln_g, ln_b, w_out, b_out):
    h = rms_norm(x, norm_g)
    proj = h @ w_in + b_in
    a, b, gate = jnp.split(proj, 3, axis=-1)
    u = a * jax.nn.sigmoid(b)
    u = lax.conv_general_dilated(
        u, conv_w[:, None, :].astype(u.dtype), window_strides=(1,),
        padding=((CONV_KERNEL - 1, 0),), dimension_numbers=('NWC', 'WIO', 'NWC'),
        feature_group_count=CONV_WIDTH) + conv_b
    u = jax.nn.silu(layer_norm(u, ln_g, ln_b))
    y = u * jax.nn.silu(gate)
    return x + y @ w_out + b_out


def setup_inputs(seed: int = 0) -> dict:
    key = jax.random.key(seed)
    ks = jax.random.split(key, 16)
    f32 = jnp.float32
    nrm = lambda k, shape, scale: jax.random.normal(k, shape, f32) * scale
    return {
        "x": nrm(ks[0], (BATCH, SEQ, D_MODEL), 1.0),
        "norm_even": 1.0 + nrm(ks[1], (N_EVEN, D_MODEL), 0.02),
        "w_in_even": nrm(ks[2], (N_EVEN, D_MODEL, EVEN_IN), D_MODEL ** -0.5),
        "w_out_even": nrm(ks[3], (N_EVEN, EVEN_WIDTH, D_MODEL), EVEN_WIDTH ** -0.5),
        "norm_odd": 1.0 + nrm(ks[4], (N_ODD, D_MODEL), 0.02),
        "w_in_odd": nrm(ks[5], (N_ODD, D_MODEL, 3 * CONV_WIDTH), D_MODEL ** -0.5),
        "b_in_odd": nrm(ks[6], (N_ODD, 3 * CONV_WIDTH), 0.02),
        "conv_w_odd": nrm(ks[7], (N_ODD, CONV_KERNEL, CONV_WIDTH), CONV_KERNEL ** -0.5),
        "conv_b_odd": nrm(ks[8], (N_ODD, CONV_WIDTH), 0.02),
        "ln_g_odd": 1.0 + nrm(ks[9], (N_ODD, CONV_WIDTH), 0.02),
        "ln_b_odd": nrm(ks[10], (N_ODD, CONV_WIDTH), 0.02),
        "w_out_odd": nrm(ks[11], (N_ODD, CONV_WIDTH, D_MODEL), CONV_WIDTH ** -0.5),
        "b_out_odd": nrm(ks[12], (N_ODD, D_MODEL), 0.02),
        "final_norm": 1.0 + nrm(ks[13], (D_MODEL,), 0.02),
    }


def reference(x, norm_even, w_in_even, w_out_even, norm_odd, w_in_odd, b_in_odd,
              conv_w_odd, conv_b_odd, ln_g_odd, ln_b_odd, w_out_odd, b_out_odd, final_norm):
    pos = jnp.arange(x.shape[1], dtype=jnp.float32)
    for layer in range(DEPTH):
        i = layer // 2
        if layer % 2 == 0:
            x = even_layer(x, norm_even[i], w_in_even[i], w_out_even[i], pos)
        else:
            x = odd_layer(x, norm_odd[i], w_in_odd[i], b_in_odd[i], conv_w_odd[i], conv_b_odd[i],
                          ln_g_odd[i], ln_b_odd[i], w_out_odd[i], b_out_odd[i])
    return rms_norm(x, final_norm)
```

```python
import numpy as np
import ml_dtypes
from contextlib import ExitStack
import concourse.bass as bass
import concourse.mybir as mybir
from concourse.bass_utils import run_bass_kernel_spmd

F32 = mybir.dt.float32
BF16 = mybir.dt.bfloat16
AF = mybir.ActivationFunctionType
ALU = mybir.AluOpType

D = 2048
KC = 16
NT = 17
NPT = 16
NSB = 3
NPREV = NSB * NPT
NXT = NPREV + NT
TOK = NT * 128
EPS = 1e-6
AQ0, AK0, AV0, BQ0, BK0, BV0, G0 = 0, 3072, 6144, 9216, 10240, 11264, 13312
DIL = (1, 4, 16)
HALO_T = (1, 4, 16)
KT_OFF = (0, 2304, 4992)
V_OFF = (0, 18, 39)
MI = [(g, o) for g in range(3) for o in range(HALO_T[g] + 1)]
C_COSA, C_SINA, C_KDP, C_KDO, C_EPSB, C_VAL, C_UV, C_IDF, C_ONE = 0, 528, 1056, 1440, 1448, 1456, 1489, 1490, 1618
NCT = 1619 + 127
GAM = [1.0 - 2.0 ** (-5 - h) for h in range(8)]


class Buf:
    __slots__ = ("name", "w", "r", "sem", "semv")

    def __init__(self, name):
        self.name = name
        self.w = None
        self.r = {}
        self.sem = None
        self.semv = 0


class Sched:
    ENG = ("pe", "dve", "act", "pool", "sp")

    def __init__(self, nc, stack):
        self.nc = nc
        self.stack = stack
        self.prog = {e: [] for e in self.ENG}
        self.cnt = {e: 0 for e in self.ENG}
        self.sem = {e: stack.enter_context(nc.semaphore("s_" + e)) for e in self.ENG}
        self.seen = {e: {} for e in self.ENG}
        self.dmabufs = []

    def _wait(self, e, tok):
        sem, v, key, src = tok
        if src == e and e in ("pe", "sp"):
            return
        if self.seen[e].get(key, 0) >= v:
            return
        self.seen[e][key] = v
        self.prog[e].append(lambda eng, sem=sem, v=v: eng.wait_ge(sem, v))

    def _deps(self, e, reads, writes):
        for b in reads:
            if b.w is not None:
                self._wait(e, b.w)
        for b in writes:
            if b.w is not None:
                self._wait(e, b.w)
            for t in b.r.values():
                self._wait(e, t)

    def _mark(self, tok, reads, writes):
        for b in writes:
            b.w = tok
            b.r = {}
        for b in reads:
            b.r[tok[2]] = tok

    def op(self, e, fn, reads=(), writes=()):
        psr = [x for x in reads if x.name.startswith("ps")]
        if psr:
            reads = [x for x in reads if not x.name.startswith("ps")]
            writes = list(writes) + psr
        self._deps(e, reads, writes)
        self.cnt[e] += 1
        sem = self.sem[e]
        self.prog[e].append(lambda eng, fn=fn, sem=sem: fn(eng).then_inc(sem, 1))
        self._mark((sem, self.cnt[e], e, e), reads, writes)

    def dma(self, q, out, in_, reads=(), writes=(), owner=None):
        self._deps(q, reads, writes)
        if owner is None:
            owner = writes[0] if writes else reads[0]
        if owner.sem is None:
            owner.sem = self.stack.enter_context(self.nc.semaphore("d_" + owner.name))
            self.dmabufs.append(owner)
        owner.semv += 16
        sem = owner.sem
        self.prog[q].append(lambda eng, out=out, in_=in_, sem=sem: eng.dma_start(out=out, in_=in_).then_inc(sem, 16))
        self._mark((sem, owner.semv, "d_" + owner.name, "dma"), reads, writes)

    def barrier(self):
        for e in self.ENG:
            for f in self.ENG:
                if f != e and self.cnt[f] > 0:
                    self._wait(e, (self.sem[f], self.cnt[f], f, f))
            for b in self.dmabufs:
                self._wait(e, (b.sem, b.semv, "d_" + b.name, "dma"))

    def finish(self):
        self.barrier()
        with self.nc.Block() as block:
            for e, deco in (("pe", block.tensor), ("dve", block.vector), ("act", block.scalar),
                            ("pool", block.gpsimd), ("sp", block.sync)):
                def body(eng, prog=self.prog[e]):
                    for f in prog:
                        f(eng)
                deco(body)


class StopBuild(Exception):
    pass


def build(stage=9, cp_stop=None):
    try:
        return _build(stage, cp_stop)
    except StopBuild as e:
        return e.args[0]


def _build(stage, cp_stop):
    nc = bass.Bass("TRN2", target_bir_lowering=False)

    def din(name, shape, dt=F32):
        return nc.dram_tensor(name, shape, dt, kind="ExternalInput").ap()

    def dscr(name, shape, dt):
        return nc.dram_tensor(name, shape, dt, kind="Internal").ap()

    xc = din("xc", [NXT * 128, D])
    w_in_e = din("w_in_e", [D, 16384] if stage != 5 else [128, 128])
    w_out_e = din("w_out_e", [3072, D] if stage != 5 else [128, 128])
    w_in_o = din("w_in_o", [D, 12288] if stage >= 5 else [128, 128])
    w_out_o = din("w_out_o", [4096, D] if stage >= 5 else [128, 128])
    g_e = din("g_e", [D])
    g_o = din("g_o", [D])
    g_f = din("g_f", [D])
    b_o = din("b_o", [D])
    c1d = din("c1", [128, 32 * 37])
    ctabd = din("ctab", [128, NCT])
    tabBd = din("tabB", [128, NXT, 128])
    cbfd = din("cbf", [128, 3200], BF16)
    outd = nc.dram_tensor("out", [2048, D], F32, kind="ExternalOutput").ap()
    x1s = nc.dram_tensor("x1s", [TOK, D], F32, kind={1: "ExternalOutput", 5: "ExternalInput"}.get(stage, "Internal")).ap()
    khalo = dscr("khalo", [8, 128, 2688], BF16)
    vhalo = dscr("vhalo", [8, 128, 21, 128], BF16)
    yTs = dscr("yTs", [24, 128, TOK], BF16)
    vS = dscr("vS", [32, 128, 2048], F32)
    sgS = dscr("sgS", [32, 128, 2048], BF16)
    yT1s = dscr("yT1s", [32, 128, 2048], BF16)
    x2s = dscr("x2s", [2048, D], F32)

    with ExitStack() as st:
        S = Sched(nc, st)

        def sb(name, shape, dt):
            return st.enter_context(nc.sbuf_tensor(name, shape, dt))

        def ps(name, shape, dt):
            return st.enter_context(nc.psum_tensor(name, shape, dt))

        BIG = sb("BIG", [128, KC * TOK], BF16)
        hT = BIG[:, :].rearrange("p (k t) -> p k t", k=KC)
        wb = [sb("wb%d" % i, [128, 8192], BF16) for i in range(2)]
        xt = [sb("xt%d" % i, [128, D], F32) for i in range(2)]
        xn = sb("xn", [128, D], BF16)
        gb = sb("gb", [128, D], F32)
        ctab = sb("ctab_s", [128, NCT], F32)
        cbf = sb("cbf_s", [128, 3200], BF16)
        tB = [sb("tB%d" % i, [128, 128], F32) for i in range(2)]
        UNI = sb("UNI", [128, 27024], BF16)
        KT = UNI[:, 0:9216]
        Vb = UNI[:, 9216:18576].rearrange("p (s c) -> p s c", s=72)
        yTu = UNI[:, 18576:22928].rearrange("p (e t) -> p e t", e=2)
        Sf = UNI[:, 22928:27024].bitcast(F32).rearrange("p (h c) -> p h c", h=8)
        PT = [sb("PT%d" % i, [128, 512], BF16) for i in range(3)]
        small = sb("small", [128, 64], F32)
        tm = sb("tm", [128, 512], BF16)
        kb16 = sb("kb16", [128, 128], BF16)
        rp = sb("rp", [128, 5, 128], F32)
        QT = sb("QT", [128, 4, 128], BF16)
        sg = sb("sg", [128, 256], F32)
        ytm = sb("ytm", [128, 256], BF16)
        Sb16 = sb("Sb16", [128, 256], BF16)
        vb16 = sb("vb16", [128, 256], BF16)

        psP = [ps("psP%d" % i, [128, 512], F32) for i in range(2)]
        psS = [ps("psS%d" % i, [128, 512], F32) for i in range(2)]
        psO = [ps("psO%d" % i, [128, 512], F32) for i in range(2)]
        psT = [ps("psT%d" % i, [128, 8, 128], BF16) for i in range(2)]

        B = {}

        def b(name):
            if name not in B:
                B[name] = Buf(name)
            return B[name]

        identb = cbf[:, 0:128]
        identf = ctab[:, C_IDF:C_IDF + 128]
        TM = [b("tm0"), b("tm1"), b("tm2")]
        ALLPS = [b("psP0"), b("psP1"), b("psS0"), b("psS1"), b("psO0"), b("psO1")]
        cnt = {"P": 0, "S": 0, "O": 0, "T": 0, "W": 0, "X": 0, "PT": 0, "TB": 0}

        def nxt(k, n):
            cnt[k] += 1
            return cnt[k] % n

        S.dma("sp", ctab[:], ctabd[:, :], writes=[b("ctab")])
        S.dma("sp", cbf[:], cbfd[:, :], writes=[b("cbf")])
        S.op("dve", lambda e: e.memset(Vb[:, :, 128:130], 0.0), writes=[b("Vb")])
        for g in range(3):
            for s_ in range(HALO_T[g] + NT):
                t33 = 16 - HALO_T[g] + s_
                S.op("dve", lambda e, g=g, s_=s_, t33=t33: e.tensor_copy(
                    Vb[:, V_OFF[g] + s_, 128:129], ctab[:, C_VAL + t33:C_VAL + t33 + 1]),
                    reads=[b("ctab")], writes=[b("Vb")])

        def col(c0, n=1):
            return ctab[:, c0:c0 + n]

        def cp(n):
            if cp_stop is not None and n >= cp_stop:
                S.finish()
                raise StopBuild(nc)

        cp(1)

        def dbg(n):
            if stage == 0 and n >= cp_stop:
                S.finish()
                raise StopBuild(nc)
        def load_w(dst_view, src_ap, bw):
            S.dma("pool", dst_view, src_ap.rearrange("(k p) c -> p k c", p=128), writes=[bw])

        def build_hT(src, r0, ntiles, gvec):
            S.dma("sp", gb[:], gvec.partition_broadcast(128), writes=[b("gb")])
            for i in range(ntiles):
                xi = nxt("X", 2)
                S.dma("sp", xt[xi][:], src[r0 + i * 128:r0 + (i + 1) * 128, :], writes=[b("xt%d" % xi)])
                S.op("act", lambda e, xi=xi: e.activation(xn[:], xt[xi][:], AF.Square, accum_out=small[:, 0:1]),
                     reads=[b("xt%d" % xi)], writes=[b("xn"), b("ss")])
                S.op("dve", lambda e: e.tensor_scalar(small[:, 1:2], small[:, 0:1], 1.0 / D, EPS, ALU.mult, ALU.add),
                     reads=[b("ss")], writes=[b("ss1")])
                S.op("act", lambda e: e.activation(small[:, 2:3], small[:, 1:2], AF.Sqrt), reads=[b("ss1")], writes=[b("ss2")])
                S.op("dve", lambda e: e.reciprocal(small[:, 3:4], small[:, 2:3]), reads=[b("ss2")], writes=[b("rstd")])
                S.op("dve", lambda e, xi=xi: e.scalar_tensor_tensor(xn[:], xt[xi][:], small[:, 3:4], gb[:], ALU.mult, ALU.mult),
                     reads=[b("xt%d" % xi), b("rstd"), b("gb")], writes=[b("xn")])
                for half in range(2):
                    ti = nxt("T", 2)
                    for k in range(8):
                        kc = half * 8 + k
                        S.op("pe", lambda e, ti=ti, k=k, kc=kc: e.transpose(psT[ti][:, k, :], xn[:, kc * 128:(kc + 1) * 128], identb),
                             reads=[b("xn"), b("cbf")], writes=[b("psT%d" % ti)] + ALLPS)
                    S.op("act", lambda e, ti=ti, half=half, i=i: e.copy(hT[:, half * 8:(half + 1) * 8, i * 128:(i + 1) * 128], psT[ti][:, :, :]),
                         reads=[b("psT%d" % ti)], writes=[b("hT%d" % i)])

        def proj_tm(i, wv, ncols, bw):
            pi = nxt("P", 2)
            for kc in range(KC):
                S.op("pe", lambda e, pi=pi, kc=kc, i=i: e.matmul(psP[pi][:, 0:ncols], hT[:, kc, i * 128:(i + 1) * 128], wv[:, kc, :],
                                                              start=(kc == 0), stop=(kc == KC - 1)),
                     reads=[b("hT%d" % i), bw], writes=[b("psP%d" % pi)])
            return pi

        def rope(pi, c0, nb, half, cos_ap, sin_ap, dst, bdst, rdeps):
            src = psP[pi][:, c0:c0 + nb * 128].rearrange("p (n c) -> p n c", n=nb)
            x1 = src[:, :, 0:half]
            x2 = src[:, :, half:2 * half]
            cb_ = cos_ap.unsqueeze(1).to_broadcast([128, nb, half])
            sb_ = sin_ap.unsqueeze(1).to_broadcast([128, nb, half])
            n = nb * half
            t = [rp[:, j, 0:n].rearrange("p (n c) -> p n c", n=nb) for j in range(4)]
            bp = b("psP%d" % pi)
            S.op("dve", lambda e: e.tensor_tensor(t[0], x1, cb_, ALU.mult), reads=[bp] + rdeps, writes=[b("rp0")])
            S.op("dve", lambda e: e.tensor_tensor(t[1], x2, sb_, ALU.mult), reads=[bp] + rdeps, writes=[b("rp1")])
            S.op("dve", lambda e: e.tensor_tensor(t[2], x2, cb_, ALU.mult), reads=[bp] + rdeps, writes=[b("rp2")])
            S.op("dve", lambda e: e.tensor_tensor(t[3], x1, sb_, ALU.mult), reads=[bp] + rdeps, writes=[b("rp3")])
            S.op("dve", lambda e: e.tensor_tensor(dst[:, :, 0:half], t[0], t[1], ALU.subtract),
                 reads=[b("rp0"), b("rp1")], writes=bdst)
            S.op("dve", lambda e: e.tensor_tensor(dst[:, :, half:2 * half], t[2], t[3], ALU.add),
                 reads=[b("rp2"), b("rp3")], writes=bdst)

        def load_tB(t65):
            ti = nxt("TB", 2)
            S.dma("sp", tB[ti][:], tabBd[:, t65, :], writes=[b("tB%d" % ti)])
            return ti

        def rope_b(pi, c0, t_i, dst, bdst, scale):
            src = psP[pi][:, c0:c0 + 128]
            x1, x2 = src[:, 0:64], src[:, 64:128]
            cs, sn = tB[t_i][:, 0:64], tB[t_i][:, 64:128]
            bp, bt = b("psP%d" % pi), b("tB%d" % t_i)
            o = rp[:, 4, 0:128]
            t = [rp[:, j, 0:64] for j in range(4)]
            S.op("dve", lambda e: e.tensor_tensor(t[0], x1, cs, ALU.mult), reads=[bp, bt], writes=[b("rp0")])
            S.op("dve", lambda e: e.tensor_tensor(t[1], x2, sn, ALU.mult), reads=[bp, bt], writes=[b("rp1")])
            S.op("dve", lambda e: e.tensor_tensor(t[2], x2, cs, ALU.mult), reads=[bp, bt], writes=[b("rp2")])
            S.op("dve", lambda e: e.tensor_tensor(t[3], x1, sn, ALU.mult), reads=[bp, bt], writes=[b("rp3")])
            S.op("dve", lambda e: e.tensor_tensor(o[:, 0:64], t[0], t[1], ALU.subtract), reads=[b("rp0"), b("rp1")], writes=[b("rp4")])
            S.op("dve", lambda e: e.tensor_tensor(o[:, 64:128], t[2], t[3], ALU.add), reads=[b("rp2"), b("rp3")], writes=[b("rp4")])
            if scale is None:
                S.op("act", lambda e: e.copy(dst, o), reads=[b("rp4")], writes=[bdst])
            else:
                S.op("act", lambda e: e.activation(dst, o, AF.Copy, scale=scale), reads=[b("rp4"), b("ctab")], writes=[bdst])

        def a_kv_part(h, tiles, t33_of, slot_of, bw):
            wv = [wb[j][:, 0:KC * 384].rearrange("p (k c) -> p k c", k=KC) for j in range(2)]
            for part, c_base in ((0, AK0), (1, AV0)):
                bwp = b("wb%d" % part)
                for g in range(3):
                    load_w(wv[part][:, :, g * 128:(g + 1) * 128], w_in_e[:, c_base + g * 1024 + h * 128: c_base + g * 1024 + (h + 1) * 128], bwp)
            for i in tiles:
                pi = proj_tm(i, wv[0], 384, b("wb0"))
                tmv = tm[:, 0:384].rearrange("p (n c) -> p n c", n=3)
                S.op("act", lambda e, pi=pi: e.copy(tm[:, 0:384], psP[pi][:, 0:384]), reads=[b("psP%d" % pi)], writes=TM)
                t33 = t33_of(i)
                dbg(0.5)
                rope(pi, 0, 3, 16, col(C_COSA + t33 * 16, 16), col(C_SINA + t33 * 16, 16), tmv, TM, [b("ctab")])
                dbg(1)
                ti = nxt("T", 2)
                for g in range(3):
                    S.op("pe", lambda e, ti=ti, g=g: e.transpose(psT[ti][:, g, :], tm[:, g * 128:(g + 1) * 128], identb),
                         reads=TM + [b("cbf")], writes=[b("psT%d" % ti)] + ALLPS)
                    dbg(1.1 + 0.1 * g)
                dbg(1.5)
                for g in range(3):
                    sl = slot_of(g, i)
                    if sl is None:
                        continue
                    S.op("act", lambda e, ti=ti, g=g, sl=sl: e.copy(KT[:, KT_OFF[g] + sl * 128: KT_OFF[g] + (sl + 1) * 128], psT[ti][:, g, :]),
                         reads=[b("psT%d" % ti)], writes=[b("KT")])
                dbg(2)
                pi = proj_tm(i, wv[1], 384, b("wb1"))
                for g in range(3):
                    sl = slot_of(g, i)
                    if sl is None:
                        continue
                    S.op("act", lambda e, pi=pi, g=g, sl=sl: e.copy(Vb[:, V_OFF[g] + sl, 0:128], psP[pi][:, g * 128:(g + 1) * 128]),
                         reads=[b("psP%d" % pi)], writes=[b("Vb")])

        if stage == 0:
            build_hT(xc, 0, 1, g_e)
            a_kv_part(0, range(1), lambda i: i, lambda g, i: 0, None)
            dbg(0)
        def layer0():
            S.op("dve", lambda e: e.memset(Sf[:, :, :], 0.0), writes=[b("Sf")])
            for p in range(NSB):
                if stage == 2:
                    break
                S.barrier()
                build_hT(xc, p * NPT * 128, NPT, g_e)
                cp(2)
                for h in range(8):
                    wv = wb[h % 2][:, 0:KC * 384].rearrange("p (k c) -> p k c", k=KC)
                    bw = b("wb%d" % (h % 2))
                    load_w(wv[:, :, 0:128], w_in_e[:, BK0 + h * 128:BK0 + (h + 1) * 128], bw)
                    load_w(wv[:, :, 128:384], w_in_e[:, BV0 + h * 256:BV0 + (h + 1) * 256], bw)
                    for i in range(NPT):
                        t48 = p * NPT + i
                        pi = proj_tm(i, wv, 384, bw)
                        t_i = load_tB(t48)
                        rope_b(pi, 0, t_i, kb16[:], b("kb16"), col(C_KDP + t48 * 8 + h))
                        S.op("act", lambda e, pi=pi: e.copy(vb16[:], psP[pi][:, 128:384]), reads=[b("psP%d" % pi)], writes=[b("vb16")])
                        S.op("pe", lambda e, i=i: e.matmul(psO[0][:, 0:256], kb16[:], vb16[:], start=(i == 0), stop=(i == NPT - 1)),
                             reads=[b("kb16"), b("vb16")], writes=[b("psO0")])
                    S.op("dve", lambda e, h=h: e.tensor_tensor(Sf[:, h, :], psO[0][:, 0:256], Sf[:, h, :], ALU.add),
                         reads=[b("psO0"), b("Sf")], writes=[b("Sf")])
                    cp(3)
                cp(3.2 + 0.1 * p)
                if p == NSB - 1:
                    cp(3.5)
                    for h in range(8):
                        a_kv_part(h, range(NPT), lambda i: i,
                                  lambda g, i: (i - (NPT - HALO_T[g])) if i >= NPT - HALO_T[g] else None, None)
                        cp(3.7)
                        for g in range(3):
                            hc = HALO_T[g] * 128
                            ko = (0, 128, 640)[g]
                            S.dma("sp", khalo[h, :, ko:ko + hc], KT[:, KT_OFF[g]:KT_OFF[g] + hc], reads=[b("KT")], writes=[b("khalo")], owner=b("KT"))
                            vo = (0, 1, 5)[g]
                            S.dma("sp", vhalo[h, :, vo:vo + HALO_T[g], :], Vb[:, V_OFF[g]:V_OFF[g] + HALO_T[g], 0:128],
                                  reads=[b("Vb")], writes=[b("vhalo")], owner=b("Vb"))
                        cp(3.8)

            cp(4)
            S.barrier()
            build_hT(xc, NPREV * 128, NT, g_e)
            cp(5)
            inv_sqrt = 128.0 ** -0.5
            for h in range(8):
                for g in range(3):
                    hc = HALO_T[g] * 128
                    ko = (0, 128, 640)[g]
                    vo = (0, 1, 5)[g]
                    S.dma("sp", KT[:, KT_OFF[g]:KT_OFF[g] + hc], khalo[h, :, ko:ko + hc], reads=[b("khalo")], writes=[b("KT")])
                    S.dma("sp", Vb[:, V_OFF[g]:V_OFF[g] + HALO_T[g], 0:128], vhalo[h, :, vo:vo + HALO_T[g], :], reads=[b("vhalo")], writes=[b("Vb")])
                a_kv_part(h, range(NT), lambda i: 16 + i, lambda g, i: HALO_T[g] + i, None)
                wq = wb[0][:, 0:KC * 512].rearrange("p (k c) -> p k c", k=KC)
                for g in range(3):
                    load_w(wq[:, :, g * 128:(g + 1) * 128], w_in_e[:, AQ0 + g * 1024 + h * 128:AQ0 + g * 1024 + (h + 1) * 128], b("wb0"))
                load_w(wq[:, :, 384:512], w_in_e[:, G0 + h * 128:G0 + (h + 1) * 128], b("wb0"))
                for i in range(NT):
                    pi = proj_tm(i, wq, 512, b("wb0"))
                    tmv = tm[:, 0:384].rearrange("p (n c) -> p n c", n=3)
                    S.op("act", lambda e, pi=pi: e.copy(tm[:, 0:384], psP[pi][:, 0:384]), reads=[b("psP%d" % pi)], writes=TM)
                    S.op("act", lambda e, pi=pi: e.activation(sg[:, 0:128], psP[pi][:, 384:512], AF.Silu), reads=[b("psP%d" % pi)], writes=[b("sg")])
                    t33 = 16 + i
                    rope(pi, 0, 3, 16, col(C_COSA + t33 * 16, 16), col(C_SINA + t33 * 16, 16), tmv, TM, [b("ctab")])
                    ti = nxt("T", 2)
                    for g in range(3):
                        S.op("pe", lambda e, ti=ti, g=g: e.transpose(psT[ti][:, g, :], tm[:, g * 128:(g + 1) * 128], identb),
                             reads=TM + [b("cbf")], writes=[b("psT%d" % ti)] + ALLPS)
                    S.op("act", lambda e, ti=ti: e.copy(QT[:, 0:3, :], psT[ti][:, 0:3, :]), reads=[b("psT%d" % ti)], writes=[b("QT")])
                    oi = nxt("O", 2)
                    for ch in range(6):
                        si = nxt("S", 2)
                        pti = nxt("PT", 3)
                        for j in range(4):
                            g, o = MI[ch * 4 + j]
                            sl = HALO_T[g] + i - o
                            S.op("pe", lambda e, si=si, j=j, g=g, sl=sl: e.matmul(
                                psS[si][:, j * 128:(j + 1) * 128], KT[:, KT_OFF[g] + sl * 128:KT_OFF[g] + (sl + 1) * 128], QT[:, g, :],
                                start=True, stop=True), reads=[b("KT"), b("QT")], writes=[b("psS%d" % si)])
                        S.op("act", lambda e, si=si, pti=pti: e.activation(PT[pti][:], psS[si][:], AF.Exp, scale=inv_sqrt),
                             reads=[b("psS%d" % si)], writes=[b("PT%d" % pti)])
                        S.op("dve", lambda e, pti=pti, ch=ch: e.tensor_tensor(PT[pti][:], PT[pti][:], cbf[:, 128 + ch * 512:128 + (ch + 1) * 512], ALU.mult),
                             reads=[b("PT%d" % pti), b("cbf")], writes=[b("PT%d" % pti)])
                        for j in range(4):
                            g, o = MI[ch * 4 + j]
                            sl = HALO_T[g] + i - o
                            S.op("pe", lambda e, oi=oi, pti=pti, j=j, g=g, sl=sl, ch=ch: e.matmul(
                                psO[oi][:, 0:129], PT[pti][:, j * 128:(j + 1) * 128], Vb[:, V_OFF[g] + sl, 0:129],
                                start=(ch == 0 and j == 0), stop=(ch == 5 and j == 3)),
                                reads=[b("PT%d" % pti), b("Vb")], writes=[b("psO%d" % oi)])
                    S.op("dve", lambda e, oi=oi: e.tensor_scalar_max(small[:, 8:9], psO[oi][:, 128:129], 1e-30),
                         reads=[b("psO%d" % oi)], writes=[b("den")])
                    S.op("dve", lambda e: e.reciprocal(small[:, 9:10], small[:, 8:9]), reads=[b("den")], writes=[b("rden")])
                    S.op("dve", lambda e, oi=oi: e.scalar_tensor_tensor(ytm[:, 0:128], psO[oi][:, 0:128], small[:, 9:10], sg[:, 0:128], ALU.mult, ALU.mult),
                         reads=[b("psO%d" % oi), b("rden"), b("sg")], writes=[b("ytm")])
                    ti = nxt("T", 2)
                    S.op("pe", lambda e, ti=ti: e.transpose(psT[ti][:, 0, :], ytm[:, 0:128], identb), reads=[b("ytm"), b("cbf")], writes=[b("psT%d" % ti)] + ALLPS)
                    S.op("act", lambda e, ti=ti, i=i: e.copy(yTu[:, 0, i * 128:(i + 1) * 128], psT[ti][:, 0, :]), reads=[b("psT%d" % ti)], writes=[b("yTu")])
                S.dma("sp", yTs[h, :, :], yTu[:, 0, :], reads=[b("yTu")], writes=[b("yTs")], owner=b("yTu"))
                cp(6)
            cp(7)

            for h in range(8):
                wq = wb[0][:, 0:KC * 512].rearrange("p (k c) -> p k c", k=KC)
                wg = wb[1][:, 0:KC * 256].rearrange("p (k c) -> p k c", k=KC)
                load_w(wq[:, :, 0:128], w_in_e[:, BQ0 + h * 128:BQ0 + (h + 1) * 128], b("wb0"))
                load_w(wq[:, :, 128:256], w_in_e[:, BK0 + h * 128:BK0 + (h + 1) * 128], b("wb0"))
                load_w(wq[:, :, 256:512], w_in_e[:, BV0 + h * 256:BV0 + (h + 1) * 256], b("wb0"))
                load_w(wg, w_in_e[:, G0 + 1024 + h * 256:G0 + 1024 + (h + 1) * 256], b("wb1"))
                S.op("act", lambda e, h=h: e.copy(Sb16[:], Sf[:, h, :]), reads=[b("Sf")], writes=[b("Sb16")])
                cg = GAM[h] ** 128
                for i in range(NT):
                    pi = proj_tm(i, wq, 512, b("wb0"))
                    t_i = load_tB(NPREV + i)
                    rope_b(pi, 0, t_i, tm[:, 0:128], b("tm0"), None)
                    rope_b(pi, 128, t_i, tm[:, 128:256], b("tm1"), col(C_KDO + h))
                    S.op("act", lambda e: e.copy(kb16[:], tm[:, 128:256]), reads=[b("tm1")], writes=[b("kb16")])
                    S.op("act", lambda e, pi=pi: e.copy(vb16[:], psP[pi][:, 256:512]), reads=[b("psP%d" % pi)], writes=[b("vb16")])
                    pg = proj_tm(i, wg, 256, b("wb1"))
                    S.op("act", lambda e, pg=pg: e.activation(sg[:, 0:256], psP[pg][:, 0:256], AF.Silu), reads=[b("psP%d" % pg)], writes=[b("sg")])
                    ti = nxt("T", 2)
                    S.op("pe", lambda e, ti=ti: e.transpose(psT[ti][:, 0, :], tm[:, 0:128], identb), reads=[b("tm0"), b("cbf")], writes=[b("psT%d" % ti)] + ALLPS)
                    S.op("pe", lambda e, ti=ti: e.transpose(psT[ti][:, 1, :], tm[:, 128:256], identb), reads=[b("tm1"), b("cbf")], writes=[b("psT%d" % ti)] + ALLPS)
                    S.op("act", lambda e, ti=ti: e.copy(QT[:, 0:2, :], psT[ti][:, 0:2, :]), reads=[b("psT%d" % ti)], writes=[b("QT")])
                    S.op("pe", lambda e: e.matmul(psS[0][:, 0:128], QT[:, 1, :], QT[:, 0, :], start=True, stop=True),
                         reads=[b("QT")], writes=[b("psS0")])
                    S.op("dve", lambda e: e.tensor_tensor(PT[0][:, 0:128], psS[0][:, 0:128], cbf[:, 128:256], ALU.mult),
                         reads=[b("psS0"), b("cbf")], writes=[b("PT0")])
                    oi = nxt("O", 2)
                    S.op("pe", lambda e, oi=oi: e.matmul(psO[oi][:, 0:256], PT[0][:, 0:128], vb16[:], start=True, stop=False),
                         reads=[b("PT0"), b("vb16")], writes=[b("psO%d" % oi)])
                    S.op("pe", lambda e, oi=oi: e.matmul(psO[oi][:, 0:256], QT[:, 0, :], Sb16[:], start=False, stop=True),
                         reads=[b("QT"), b("Sb16")], writes=[b("psO%d" % oi)])
                    S.op("pe", lambda e: e.matmul(psS[1][:, 0:256], kb16[:], vb16[:], start=True, stop=True),
                         reads=[b("kb16"), b("vb16")], writes=[b("psS1")])
                    S.op("act", lambda e, oi=oi: e.activation(xn[:, 0:256], psO[oi][:, 0:256], AF.Square, accum_out=small[:, 12:13]),
                         reads=[b("psO%d" % oi)], writes=[b("xn"), b("zss")])
                    S.op("dve", lambda e, h=h: e.tensor_scalar(small[:, 13:14], small[:, 12:13], 1.0 / 256, col(C_EPSB + h), ALU.mult, ALU.add),
                         reads=[b("zss"), b("ctab")], writes=[b("zs1")])
                    S.op("act", lambda e: e.activation(small[:, 14:15], small[:, 13:14], AF.Sqrt), reads=[b("zs1")], writes=[b("zs2")])
                    S.op("dve", lambda e: e.reciprocal(small[:, 15:16], small[:, 14:15]), reads=[b("zs2")], writes=[b("zr")])
                    S.op("dve", lambda e, oi=oi: e.scalar_tensor_tensor(ytm[:, 0:256], psO[oi][:, 0:256], small[:, 15:16], sg[:, 0:256], ALU.mult, ALU.mult),
                         reads=[b("psO%d" % oi), b("zr"), b("sg")], writes=[b("ytm")])
                    ti = nxt("T", 2)
                    for e2 in range(2):
                        S.op("pe", lambda e, ti=ti, e2=e2: e.transpose(psT[ti][:, e2, :], ytm[:, e2 * 128:(e2 + 1) * 128], identb),
                             reads=[b("ytm"), b("cbf")], writes=[b("psT%d" % ti)] + ALLPS)
                    S.op("act", lambda e, ti=ti, i=i: e.copy(yTu[:, :, i * 128:(i + 1) * 128], psT[ti][:, 0:2, :]), reads=[b("psT%d" % ti)], writes=[b("yTu")])
                    S.op("dve", lambda e, h=h: e.tensor_tensor(Sf[:, h, :], psS[1][:, 0:256], Sf[:, h, :], ALU.add),
                         reads=[b("psS1"), b("Sf")], writes=[b("Sf")])
                    S.op("dve", lambda e, h=h, cg=cg: e.tensor_scalar_mul(Sf[:, h, :], Sf[:, h, :], cg), reads=[b("Sf")], writes=[b("Sf")])
                    S.op("act", lambda e, h=h: e.copy(Sb16[:], Sf[:, h, :]), reads=[b("Sf")], writes=[b("Sb16")])
                for e2 in range(2):
                    S.dma("sp", yTs[8 + 2 * h + e2, :, :], yTu[:, e2, :], reads=[b("yTu")], writes=[b("yTs")], owner=b("yTu"))

            cp(8)
            S.barrier()
            for (t0, t1) in ((0, 9), (9, 17)):
                S.barrier()
                ntk = (t1 - t0) * 128
                yv = BIG[:, 0:24 * ntk].rearrange("p (k t) -> p k t", k=24)
                for k in range(24):
                    S.dma("sp", yv[:, k, :], yTs[k, :, t0 * 128:t1 * 128], reads=[b("yTs")], writes=[b("yv")])
                for cbk in range(8):
                    wi = nxt("W", 2)
                    wv = wb[wi][:, 0:24 * 256].rearrange("p (k c) -> p k c", k=24)
                    load_w(wv, w_out_e[:, cbk * 256:(cbk + 1) * 256], b("wb%d" % wi))
                    for i in range(t0, t1):
                        xi = nxt("X", 2)
                        S.dma("sp", xt[xi][:, 0:256], xc[(NPREV + i) * 128:(NPREV + i + 1) * 128, cbk * 256:(cbk + 1) * 256], writes=[b("xt%d" % xi)])
                        pi = nxt("P", 2)
                        for k in range(24):
                            S.op("pe", lambda e, pi=pi, k=k, i=i, wv=wv, yv=yv, t0=t0: e.matmul(psP[pi][:, 0:256], yv[:, k, (i - t0) * 128:(i - t0 + 1) * 128], wv[:, k, :],
                                                                          start=(k == 0), stop=(k == 23)),
                                 reads=[b("yv"), b("wb%d" % wi)], writes=[b("psP%d" % pi)])
                        S.op("dve", lambda e, pi=pi, xi=xi: e.tensor_tensor(xt[xi][:, 256:512], psP[pi][:, 0:256], xt[xi][:, 0:256], ALU.add),
                             reads=[b("psP%d" % pi), b("xt%d" % xi)], writes=[b("xo%d" % xi)])
                        S.dma("sp", x1s[i * 128:(i + 1) * 128, cbk * 256:(cbk + 1) * 256], xt[xi][:, 256:512],
                              reads=[b("xo%d" % xi)], writes=[b("x1s")], owner=b("xo%d" % xi))
            S.barrier()
        if stage != 5:
            layer0()
        if stage < 5:
            S.finish()
            return nc

        ubuf = UNI[:, 0:2176]
        dg = UNI[:, 2176:6144].rearrange("p (k c) -> p k c", k=31)
        sgbuf = UNI[:, 6144:8192]
        vbuf = UNI[:, 8192:12288].bitcast(F32)
        sig = UNI[:, 12288:13312].bitcast(F32)
        Abuf = UNI[:, 13312:17408].bitcast(F32)
        Bbuf = UNI[:, 17408:21504].bitcast(F32)
        vb2 = UNI[:, 21504:22016]
        vsq = UNI[:, 22016:22528]
        c1 = UNI[:, 22528:24896].bitcast(F32)
        onesb = UNI[:, 24896:24898]
        ybuf = UNI[:, 24960:27008]
        onesf = ctab[:, C_ONE:C_ONE + 128]
        S.barrier()
        S.dma("sp", c1, c1d[:, :], writes=[b("c1")])
        S.op("dve", lambda e: e.memset(onesb, 1.0), writes=[b("onesb")])
        S.op("dve", lambda e: e.memset(PT[0][:, :], 0.0), writes=[b("PT0")])
        build_hT(x1s, 0, NT, g_o)
        S.op("pe", lambda e: e.matmul(psO[0][:, 0:32], PT[0][:, 0:128], PT[0][:, 0:32], start=True, stop=False, skip_group_check=True),
             reads=[b("PT0")], writes=[b("psO0")])

        def c1c(c):
            return c1[:, c:c + 1]

        groups = [(0, 128)] + [(128 + m * 512, 512) for m in range(4)]
        for j in range(32):
            wi = nxt("W", 2)
            bw = b("wb%d" % wi)
            wv = wb[wi][:, 0:KC * 384].rearrange("p (k c) -> p k c", k=KC)
            for part in range(3):
                load_w(wv[:, :, part * 128:(part + 1) * 128], w_in_o[:, part * 4096 + j * 128:part * 4096 + (j + 1) * 128], bw)
            for k in range(31):
                S.op("pool", lambda e, k=k, j=j: e.tensor_scalar_mul(dg[:, k, :], identb, c1c(192 + j * 31 + k)),
                     reads=[b("cbf"), b("c1")], writes=[b("dg")])
            for gi, (t0, n) in enumerate(groups):
                tl = list(range(t0 // 128, (t0 + n) // 128))
                parts = ((0, psP[0], "psP0"), (1, psP[1], "psP1")) + (((2, psS[0], "psS0"),) if gi > 0 else ())
                for part, pdst, pn in parts:
                    for kc in range(KC):
                        S.op("pe", lambda e, pdst=pdst, part=part, kc=kc, t0=t0, n=n, wv=wv: e.matmul(
                            pdst[:, 0:n], wv[:, kc, part * 128:(part + 1) * 128], hT[:, kc, t0:t0 + n], start=(kc == 0), stop=(kc == KC - 1)),
                            reads=[b("hT%d" % t) for t in tl] + [bw], writes=[b(pn)])
                S.op("act", lambda e, n=n, j=j: e.activation(sig[:, 0:n], psP[1][:, 0:n], AF.Sigmoid, bias=c1c(32 + j)),
                     reads=[b("psP1"), b("c1")], writes=[b("sig")])
                S.op("dve", lambda e, n=n, j=j, t0=t0: e.scalar_tensor_tensor(ubuf[:, t0:t0 + n], psP[0][:, 0:n], c1c(j), sig[:, 0:n], ALU.add, ALU.mult),
                     reads=[b("psP0"), b("sig"), b("c1")], writes=[b("ubuf")])
                if gi == 0:
                    S.op("dve", lambda e: e.tensor_scalar_mul(ubuf[:, 0:128], ubuf[:, 0:128], col(C_UV)),
                         reads=[b("ubuf"), b("ctab")], writes=[b("ubuf")])
                else:
                    S.op("act", lambda e, n=n, j=j, t0=t0: e.activation(sgbuf[:, t0 - 128:t0 - 128 + n], psS[0][:, 0:n], AF.Silu, bias=c1c(64 + j)),
                         reads=[b("psS0"), b("c1")], writes=[b("sgbuf")])
            for m in range(4):
                for k in range(31):
                    S.op("pe", lambda e, k=k, m=m: e.matmul(psS[1][:, 0:512], dg[:, k, :], ubuf[:, 98 + m * 512 + k:98 + m * 512 + k + 512],
                                                       start=(k == 0), stop=(k == 30)),
                         reads=[b("dg"), b("ubuf")], writes=[b("psS1")])
                S.op("act", lambda e, m=m, j=j: e.activation(vbuf[:, m * 512:(m + 1) * 512], psS[1][:, 0:512], AF.Identity, bias=c1c(96 + j)),
                     reads=[b("psS1"), b("c1")], writes=[b("vbuf")])
                S.op("act", lambda e, m=m: e.copy(vb2, vbuf[:, m * 512:(m + 1) * 512]), reads=[b("vbuf")], writes=[b("vb2")])
                S.op("act", lambda e, m=m: e.activation(vsq, vbuf[:, m * 512:(m + 1) * 512], AF.Square), reads=[b("vbuf")], writes=[b("vsq")])
                for t in range(4):
                    tile_ = m * 4 + t
                    last = (j == 31)
                    S.op("pe", lambda e, t=t, tile_=tile_, last=last: e.matmul(psO[0][:, 2 * tile_:2 * tile_ + 1], vb2[:, t * 128:(t + 1) * 128], onesb[:, 0:1],
                                                                       start=False, stop=last, skip_group_check=True),
                         reads=[b("vb2"), b("onesb")], writes=[b("psO0")])
                    S.op("pe", lambda e, t=t, tile_=tile_, last=last: e.matmul(psO[0][:, 2 * tile_ + 1:2 * tile_ + 2], vsq[:, t * 128:(t + 1) * 128], onesb[:, 0:1],
                                                                       start=False, stop=last, skip_group_check=True),
                         reads=[b("vsq"), b("onesb")], writes=[b("psO0")])
            S.dma("sp", vS[j, :, :], vbuf, reads=[b("vbuf")], writes=[b("vS")], owner=b("vbuf"))
            S.dma("sp", sgS[j, :, :], sgbuf, reads=[b("sgbuf")], writes=[b("sgS")], owner=b("sgbuf"))

        st1 = rp[:, 2, :]
        S.op("dve", lambda e: e.tensor_scalar_mul(st1[:, 0:32], psO[0][:, 0:32], 1.0 / 4096), reads=[b("psO0")], writes=[b("st1")])
        mean = st1[:, 0:32].rearrange("p (t s) -> p t s", s=2)[:, :, 0:1]
        ex2 = st1[:, 0:32].rearrange("p (t s) -> p t s", s=2)[:, :, 1:2]

        def sv(c0):
            return st1[:, c0:c0 + 16].unsqueeze(2)
        S.op("dve", lambda e: e.tensor_tensor(sv(32), mean, mean, ALU.mult), reads=[b("st1")], writes=[b("st1")])
        S.op("dve", lambda e: e.tensor_tensor(sv(48), ex2, sv(32), ALU.subtract), reads=[b("st1")], writes=[b("st1")])
        S.op("dve", lambda e: e.tensor_scalar_add(st1[:, 48:64], st1[:, 48:64], EPS), reads=[b("st1")], writes=[b("st1")])
        S.op("act", lambda e: e.activation(st1[:, 48:64], st1[:, 48:64], AF.Sqrt), reads=[b("st1")], writes=[b("st1")])
        S.op("dve", lambda e: e.reciprocal(st1[:, 64:80], st1[:, 48:64]), reads=[b("st1")], writes=[b("st1")])
        S.op("dve", lambda e: e.scalar_tensor_tensor(sv(80), mean, -1.0, sv(64), ALU.mult, ALU.mult), reads=[b("st1")], writes=[b("st1")])
        for t in range(16):
            S.op("dve", lambda e, t=t: e.tensor_scalar_mul(rp[:, 0, :], identf, st1[:, 64 + t:65 + t]), reads=[b("st1"), b("ctab")], writes=[b("rp0")])
            S.op("dve", lambda e, t=t: e.tensor_scalar_mul(rp[:, 1, :], identf, st1[:, 80 + t:81 + t]), reads=[b("st1"), b("ctab")], writes=[b("rp1")])
            S.op("pe", lambda e: e.matmul(psS[0][:, 0:128], onesf, rp[:, 0, :], start=True, stop=True), reads=[b("ctab"), b("rp0")], writes=[b("psS0")])
            S.op("pe", lambda e: e.matmul(psS[0][:, 128:256], onesf, rp[:, 1, :], start=True, stop=True), reads=[b("ctab"), b("rp1")], writes=[b("psS0")])
            S.op("act", lambda e, t=t: e.copy(Abuf[:, t * 128:(t + 1) * 128], psS[0][:, 0:128]), reads=[b("psS0")], writes=[b("Abuf")])
            S.op("act", lambda e, t=t: e.copy(Bbuf[:, t * 128:(t + 1) * 128], psS[0][:, 128:256]), reads=[b("psS0")], writes=[b("Bbuf")])
        for j in range(32):
            S.dma("sp", vbuf, vS[j, :, :], reads=[b("vS")], writes=[b("vbuf")])
            S.dma("sp", sgbuf, sgS[j, :, :], reads=[b("sgS")], writes=[b("sgbuf")])
            S.op("dve", lambda e: e.tensor_tensor(vbuf, vbuf, Abuf, ALU.mult), reads=[b("vbuf"), b("Abuf")], writes=[b("vbuf")])
            S.op("pool", lambda e: e.tensor_tensor(vbuf, vbuf, Bbuf, ALU.add), reads=[b("vbuf"), b("Bbuf")], writes=[b("vbuf")])
            S.op("act", lambda e, j=j: e.activation(vbuf, vbuf, AF.Silu, bias=c1c(160 + j), scale=c1c(128 + j)),
                 reads=[b("vbuf"), b("c1")], writes=[b("vbuf")])
            S.op("dve", lambda e: e.tensor_tensor(ybuf, vbuf, sgbuf, ALU.mult), reads=[b("vbuf"), b("sgbuf")], writes=[b("ybuf")])
            S.dma("sp", yT1s[j, :, :], ybuf, reads=[b("ybuf")], writes=[b("yT1s")], owner=b("ybuf"))

        S.barrier()
        S.dma("sp", gb[:], b_o.partition_broadcast(128), writes=[b("gb")])
        for tg in range(2):
            S.barrier()
            yv = BIG[:, 0:32 * 1024].rearrange("p (k t) -> p k t", k=32)
            for k in range(32):
                S.dma("sp", yv[:, k, :], yT1s[k, :, tg * 1024:(tg + 1) * 1024], reads=[b("yT1s")], writes=[b("yv")])
            for cbk in range(8):
                wi = nxt("W", 2)
                wv = wb[wi][:, 0:32 * 256].rearrange("p (k c) -> p k c", k=32)
                load_w(wv, w_out_o[:, cbk * 256:(cbk + 1) * 256], b("wb%d" % wi))
                for i8 in range(8):
                    tile_ = tg * 8 + i8
                    xi = nxt("X", 2)
                    S.dma("sp", xt[xi][:, 0:256], x1s[(tile_ + 1) * 128:(tile_ + 2) * 128, cbk * 256:(cbk + 1) * 256],
                          reads=[b("x1s")], writes=[b("xt%d" % xi)])
                    pi = nxt("P", 2)
                    for k in range(32):
                        S.op("pe", lambda e, pi=pi, k=k, i8=i8, wv=wv, yv=yv: e.matmul(psP[pi][:, 0:256], yv[:, k, i8 * 128:(i8 + 1) * 128], wv[:, k, :],
                                                                           start=(k == 0), stop=(k == 31)),
                             reads=[b("yv"), b("wb%d" % wi)], writes=[b("psP%d" % pi)])
                    S.op("dve", lambda e, pi=pi, xi=xi: e.tensor_tensor(xt[xi][:, 256:512], psP[pi][:, 0:256], xt[xi][:, 0:256], ALU.add),
                         reads=[b("psP%d" % pi), b("xt%d" % xi)], writes=[b("xo%d" % xi)])
                    S.op("dve", lambda e, xi=xi, cbk=cbk: e.tensor_tensor(xt[xi][:, 256:512], xt[xi][:, 256:512], gb[:, cbk * 256:(cbk + 1) * 256], ALU.add),
                         reads=[b("xo%d" % xi), b("gb")], writes=[b("xo%d" % xi)])
                    S.dma("sp", x2s[tile_ * 128:(tile_ + 1) * 128, cbk * 256:(cbk + 1) * 256], xt[xi][:, 256:512],
                          reads=[b("xo%d" % xi)], writes=[b("x2s")], owner=b("xo%d" % xi))

        S.barrier()
        S.dma("sp", gb[:], g_f.partition_broadcast(128), writes=[b("gb")])
        for tile_ in range(16):
            xi = nxt("X", 2)
            S.dma("sp", xt[xi][:], x2s[tile_ * 128:(tile_ + 1) * 128, :], reads=[b("x2s")], writes=[b("xt%d" % xi), b("xo%d" % xi)])
            S.op("act", lambda e, xi=xi: e.activation(xn[:], xt[xi][:], AF.Square, accum_out=small[:, 0:1]),
                 reads=[b("xt%d" % xi)], writes=[b("xn"), b("ss")])
            S.op("dve", lambda e: e.tensor_scalar(small[:, 1:2], small[:, 0:1], 1.0 / D, EPS, ALU.mult, ALU.add), reads=[b("ss")], writes=[b("ss1")])
            S.op("act", lambda e: e.activation(small[:, 2:3], small[:, 1:2], AF.Sqrt), reads=[b("ss1")], writes=[b("ss2")])
            S.op("dve", lambda e: e.reciprocal(small[:, 3:4], small[:, 2:3]), reads=[b("ss2")], writes=[b("rstd")])
            S.op("dve", lambda e, xi=xi: e.scalar_tensor_tensor(xt[xi][:], xt[xi][:], small[:, 3:4], gb[:], ALU.mult, ALU.mult),
                 reads=[b("xt%d" % xi), b("rstd"), b("gb")], writes=[b("xt%d" % xi)])
            S.dma("sp", outd[tile_ * 128:(tile_ + 1) * 128, :], xt[xi][:], reads=[b("xt%d" % xi)], writes=[b("out")], owner=b("xt%d" % xi))
        S.finish()
    return nc


def _f32(a):
    return np.ascontiguousarray(a, dtype=np.float32)


def make_consts(q):
    T0 = 2048 * q
    p = np.arange(128)
    ct = np.zeros((128, NCT), np.float32)
    invA = (np.float32(500000.0) ** (-np.arange(0, 32, 2, dtype=np.float32) / np.float32(32))).astype(np.float32)
    for t in range(33):
        pos = (T0 - 128 - 2048 + t * 128 + p).astype(np.float32)
        ang = pos[:, None] * invA[None, :]
        ct[:, C_COSA + t * 16:C_COSA + (t + 1) * 16] = np.cos(ang)
        ct[:, C_SINA + t * 16:C_SINA + (t + 1) * 16] = np.sin(ang)
        ct[:, C_VAL + t] = 1.0 if (T0 - 128 - 2048 + t * 128) >= 0 else 0.0
    logg = np.log1p(-np.power(2.0, -5.0 - np.arange(8, dtype=np.float64)))
    for t in range(NPREV):
        dist = (NPREV * 128 - 1) - (t * 128 + p)
        ct[:, C_KDP + t * 8:C_KDP + (t + 1) * 8] = np.exp(dist[:, None] * logg[None, :]) * (128.0 ** -0.5)
    ct[:, C_KDO:C_KDO + 8] = np.exp(-(p[:, None] + 1.0) * logg[None, :]) * (128.0 ** -0.5)
    ct[:, C_EPSB:C_EPSB + 8] = EPS * np.exp(-2.0 * (p[:, None] + 1.0) * logg[None, :])
    ct[:, C_UV] = 1.0 if q > 0 else 0.0
    ct[:, C_IDF:C_IDF + 128] = np.eye(128, dtype=np.float32)
    ct[:, C_ONE:C_ONE + 128] = 1.0
    invB = (np.float32(10000.0) ** (-np.arange(0, 128, 2, dtype=np.float32) / np.float32(128))).astype(np.float32)
    tb = np.zeros((128, NXT, 128), np.float32)
    for t in range(NXT):
        pos = (T0 - 128 - NPREV * 128 + t * 128 + p).astype(np.float32)
        ang = pos[:, None] * invB[None, :]
        tb[:, t, 0:64] = np.cos(ang)
        tb[:, t, 64:128] = np.sin(ang)
    cb = np.zeros((128, 3200), np.float32)
    cb[:, 0:128] = np.eye(128)
    kk = p[:, None]
    qq = p[None, :]
    for m, (g, o) in enumerate(MI):
        d = DIL[g]
        delta = 128 * o + qq - kk
        cb[:, 128 + m * 128:128 + (m + 1) * 128] = ((delta >= 0) & (delta % d == 0) & (delta <= 128 * d)).astype(np.float32)
    return ct, tb, cb.astype(ml_dtypes.bfloat16)


_NC_CACHE = {}


def make_in_maps(inputs):
    x = _f32(inputs["x"])
    shared = {
        "w_in_e": _f32(inputs["w_in_even"][0]), "w_out_e": _f32(inputs["w_out_even"][0]),
        "w_in_o": _f32(inputs["w_in_odd"][0]), "w_out_o": _f32(inputs["w_out_odd"][0]),
        "g_e": _f32(inputs["norm_even"][0]), "g_o": _f32(inputs["norm_odd"][0]),
        "g_f": _f32(inputs["final_norm"]), "b_o": _f32(inputs["b_out_odd"][0]),
    }
    def pj(v):
        return _f32(v).reshape(32, 128).T
    b_in = _f32(inputs["b_in_odd"][0])
    cw = _f32(inputs["conv_w_odd"][0])
    c1 = np.zeros((128, 32 * 37), np.float32)
    for idx, v in enumerate((b_in[0:4096], b_in[4096:8192], b_in[8192:12288], inputs["conv_b_odd"][0],
                             inputs["ln_g_odd"][0], inputs["ln_b_odd"][0])):
        c1[:, idx * 32:(idx + 1) * 32] = pj(v)
    c1[:, 192:] = cw.reshape(31, 32, 128).transpose(2, 1, 0).reshape(128, 32 * 31)
    shared["c1"] = c1
    maps = []
    for c in range(8):
        bi, q = c // 4, c % 4
        T0 = 2048 * q
        xcore = np.zeros((NXT * 128, D), np.float32)
        lo = T0 - 128 - NPREV * 128
        s0 = max(lo, 0)
        xcore[s0 - lo:] = x[bi, s0:T0 + 2048]
        ct, tb, cb = make_consts(q)
        m = dict(shared)
        m.update({"xc": xcore, "ctab": ct, "tabB": tb, "cbf": cb})
        maps.append(m)
    return maps


def kernel(**inputs):
    if "nc" not in _NC_CACHE:
        _NC_CACHE["nc"] = build(9)
    nc = _NC_CACHE["nc"]
    maps = make_in_maps(inputs)
    res = run_bass_kernel_spmd(nc, maps, core_ids=list(range(8)))
    out = np.zeros((2, 8192, D), np.float32)
    for c in range(8):
        bi, q = c // 4, c % 4
        out[bi, q * 2048:(q + 1) * 2048] = res.results[c]["out"]
    return out
```

```python
import numpy as np
import ml_dtypes
from contextlib import ExitStack
import concourse.bass as bass
import concourse.mybir as mybir
from concourse.bass_utils import run_bass_kernel_spmd

F32 = mybir.dt.float32
BF16 = mybir.dt.bfloat16
AF = mybir.ActivationFunctionType
ALU = mybir.AluOpType

D = 2048
KC = 16
NT = 17
NPT = 16
NSB = 3
NPREV = NSB * NPT
NXT = NPREV + NT
TOK = NT * 128
EPS = 1e-6
AQ0, AK0, AV0, BQ0, BK0, BV0, G0 = 0, 3072, 6144, 9216, 10240, 11264, 13312
DIL = (1, 4, 16)
HALO_T = (1, 4, 16)
KT_OFF = (0, 2304, 4992)
V_OFF = (0, 18, 39)
MI = [(g, o) for g in range(3) for o in range(HALO_T[g] + 1)]
C_COSA, C_SINA, C_KDP, C_KDO, C_EPSB, C_VAL, C_UV, C_IDF, C_ONE = 0, 528, 1056, 1440, 1448, 1456, 1489, 1490, 1618
NCT = 1619 + 127
GAM = [1.0 - 2.0 ** (-5 - h) for h in range(8)]


class Buf:
    __slots__ = ("name", "w", "r", "sem", "semv")

    def __init__(self, name):
        self.name = name
        self.w = None
        self.r = {}
        self.sem = None
        self.semv = 0


class Sched:
    ENG = ("pe", "dve", "act", "pool", "sp")

    def __init__(self, nc, stack):
        self.nc = nc
        self.stack = stack
        self.prog = {e: [] for e in self.ENG}
        self.cnt = {e: 0 for e in self.ENG}
        self.sem = {e: stack.enter_context(nc.semaphore("s_" + e)) for e in self.ENG}
        self.seen = {e: {} for e in self.ENG}
        self.dmabufs = []

    def _wait(self, e, tok):
        sem, v, key, src = tok
        if src == e and e in ("pe", "sp"):
            return
        if self.seen[e].get(key, 0) >= v:
            return
        self.seen[e][key] = v
        self.prog[e].append(lambda eng, sem=sem, v=v: eng.wait_ge(sem, v))

    def _deps(self, e, reads, writes):
        for b in reads:
            if b.w is not None:
                self._wait(e, b.w)
        for b in writes:
            if b.w is not None:
                self._wait(e, b.w)
            for t in b.r.values():
                self._wait(e, t)

    def _mark(self, tok, reads, writes):
        for b in writes:
            b.w = tok
            b.r = {}
        for b in reads:
            b.r[tok[2]] = tok

    def op(self, e, fn, reads=(), writes=()):
        psr = [x for x in reads if x.name.startswith("ps")]
        if psr:
            reads = [x for x in reads if not x.name.startswith("ps")]
            writes = list(writes) + psr
        self._deps(e, reads, writes)
        self.cnt[e] += 1
        sem = self.sem[e]
        self.prog[e].append(lambda eng, fn=fn, sem=sem: fn(eng).then_inc(sem, 1))
        self._mark((sem, self.cnt[e], e, e), reads, writes)

    def dma(self, q, out, in_, reads=(), writes=(), owner=None):
        self._deps(q, reads, writes)
        if owner is None:
            owner = writes[0] if writes else reads[0]
        if owner.sem is None:
            owner.sem = self.stack.enter_context(self.nc.semaphore("d_" + owner.name))
            self.dmabufs.append(owner)
        owner.semv += 16
        sem = owner.sem
        self.prog[q].append(lambda eng, out=out, in_=in_, sem=sem: eng.dma_start(out=out, in_=in_).then_inc(sem, 16))
        self._mark((sem, owner.semv, "d_" + owner.name, "dma"), reads, writes)

    def barrier(self):
        for e in self.ENG:
            for f in self.ENG:
                if f != e and self.cnt[f] > 0:
                    self._wait(e, (self.sem[f], self.cnt[f], f, f))
            for b in self.dmabufs:
                self._wait(e, (b.sem, b.semv, "d_" + b.name, "dma"))

    def finish(self):
        self.barrier()
        with self.nc.Block() as block:
            for e, deco in (("pe", block.tensor), ("dve", block.vector), ("act", block.scalar),
                            ("pool", block.gpsimd), ("sp", block.sync)):
                def body(eng, prog=self.prog[e]):
                    for f in prog:
                        f(eng)
                deco(body)


class StopBuild(Exception):
    pass


def build(stage=9, cp_stop=None):
    try:
        return _build(stage, cp_stop)
    except StopBuild as e:
        return e.args[0]


def _build(stage, cp_stop):
    nc = bass.Bass("TRN2", target_bir_lowering=False)

    def din(name, shape, dt=F32):
        return nc.dram_tensor(name, shape, dt, kind="ExternalInput").ap()

    def dscr(name, shape, dt):
        return nc.dram_tensor(name, shape, dt, kind="Internal").ap()

    xc = din("xc", [NXT * 128, D])
    w_in_e = din("w_in_e", [D, 16384] if stage != 5 else [128, 128])
    w_out_e = din("w_out_e", [3072, D] if stage != 5 else [128, 128])
    w_in_o = din("w_in_o", [D, 12288] if stage >= 5 else [128, 128])
    w_out_o = din("w_out_o", [4096, D] if stage >= 5 else [128, 128])
    g_e = din("g_e", [D])
    g_o = din("g_o", [D])
    g_f = din("g_f", [D])
    b_o = din("b_o", [D])
    c1d = din("c1", [128, 32 * 37])
    ctabd = din("ctab", [128, NCT])
    tabBd = din("tabB", [128, NXT, 128])
    cbfd = din("cbf", [128, 3200], BF16)
    outd = nc.dram_tensor("out", [2048, D], F32, kind="ExternalOutput").ap()
    x1s = nc.dram_tensor("x1s", [TOK, D], F32, kind={1: "ExternalOutput", 5: "ExternalInput"}.get(stage, "Internal")).ap()
    khalo = dscr("khalo", [8, 128, 2688], BF16)
    vhalo = dscr("vhalo", [8, 128, 21, 128], BF16)
    yTs = dscr("yTs", [24, 128, TOK], BF16)
    vS = dscr("vS", [32, 128, 2048], F32)
    sgS = dscr("sgS", [32, 128, 2048], BF16)
    yT1s = dscr("yT1s", [32, 128, 2048], BF16)
    x2s = dscr("x2s", [2048, D], F32)

    with ExitStack() as st:
        S = Sched(nc, st)

        def sb(name, shape, dt):
            return st.enter_context(nc.sbuf_tensor(name, shape, dt))

        def ps(name, shape, dt):
            return st.enter_context(nc.psum_tensor(name, shape, dt))

        BIG = sb("BIG", [128, KC * TOK], BF16)
        hT = BIG[:, :].rearrange("p (k t) -> p k t", k=KC)
        wb = [sb("wb%d" % i, [128, 8192], BF16) for i in range(2)]
        xt = [sb("xt%d" % i, [128, D], F32) for i in range(2)]
        xn = sb("xn", [128, D], BF16)
        gb = sb("gb", [128, D], F32)
        ctab = sb("ctab_s", [128, NCT], F32)
        cbf = sb("cbf_s", [128, 3200], BF16)
        tB = [sb("tB%d" % i, [128, 128], F32) for i in range(2)]
        UNI = sb("UNI", [128, 27024], BF16)
        KT = UNI[:, 0:9216]
        Vb = UNI[:, 9216:18576].rearrange("p (s c) -> p s c", s=72)
        yTu = UNI[:, 18576:22928].rearrange("p (e t) -> p e t", e=2)
        Sf = UNI[:, 22928:27024].bitcast(F32).rearrange("p (h c) -> p h c", h=8)
        PT = [sb("PT%d" % i, [128, 512], BF16) for i in range(3)]
        small = sb("small", [128, 64], F32)
        tm = sb("tm", [128, 512], BF16)
        kb16 = sb("kb16", [128, 128], BF16)
        rp = sb("rp", [128, 5, 128], F32)
        QT = sb("QT", [128, 4, 128], BF16)
        sg = sb("sg", [128, 256], F32)
        ytm = sb("ytm", [128, 256], BF16)
        Sb16 = sb("Sb16", [128, 256], BF16)
        vb16 = sb("vb16", [128, 256], BF16)

        psP = [ps("psP%d" % i, [128, 512], F32) for i in range(2)]
        psS = [ps("psS%d" % i, [128, 512], F32) for i in range(2)]
        psO = [ps("psO%d" % i, [128, 512], F32) for i in range(2)]
        psT = [ps("psT%d" % i, [128, 8, 128], BF16) for i in range(2)]

        B = {}

        def b(name):
            if name not in B:
                B[name] = Buf(name)
            return B[name]

        identb = cbf[:, 0:128]
        identf = ctab[:, C_IDF:C_IDF + 128]
        TM = [b("tm0"), b("tm1"), b("tm2")]
        ALLPS = [b("psP0"), b("psP1"), b("psS0"), b("psS1"), b("psO0"), b("psO1")]
        cnt = {"P": 0, "S": 0, "O": 0, "T": 0, "W": 0, "X": 0, "PT": 0, "TB": 0}

        def nxt(k, n):
            cnt[k] += 1
            return cnt[k] % n

        S.dma("sp", ctab[:], ctabd[:, :], writes=[b("ctab")])
        S.dma("sp", cbf[:], cbfd[:, :], writes=[b("cbf")])
        S.op("dve", lambda e: e.memset(Vb[:, :, 128:130], 0.0), writes=[b("Vb")])
        for g in range(3):
            for s_ in range(HALO_T[g] + NT):
                t33 = 16 - HALO_T[g] + s_
                S.op("dve", lambda e, g=g, s_=s_, t33=t33: e.tensor_copy(
                    Vb[:, V_OFF[g] + s_, 128:129], ctab[:, C_VAL + t33:C_VAL + t33 + 1]),
                    reads=[b("ctab")], writes=[b("Vb")])

        def col(c0, n=1):
            return ctab[:, c0:c0 + n]

        def cp(n):
            if cp_stop is not None and n >= cp_stop:
                S.finish()
                raise StopBuild(nc)

        cp(1)

        def dbg(n):
            if stage == 0 and n >= cp_stop:
                S.finish()
                raise StopBuild(nc)
        def load_w(dst_view, src_ap, bw):
            S.dma("pool", dst_view, src_ap.rearrange("(k p) c -> p k c", p=128), writes=[bw])

        def build_hT(src, r0, ntiles, gvec):
            S.dma("sp", gb[:], gvec.partition_broadcast(128), writes=[b("gb")])
            for i in range(ntiles):
                xi = nxt("X", 2)
                S.dma("sp", xt[xi][:], src[r0 + i * 128:r0 + (i + 1) * 128, :], writes=[b("xt%d" % xi)])
                S.op("act", lambda e, xi=xi: e.activation(xn[:], xt[xi][:], AF.Square, accum_out=small[:, 0:1]),
                     reads=[b("xt%d" % xi)], writes=[b("xn"), b("ss")])
                S.op("dve", lambda e: e.tensor_scalar(small[:, 1:2], small[:, 0:1], 1.0 / D, EPS, ALU.mult, ALU.add),
                     reads=[b("ss")], writes=[b("ss1")])
                S.op("act", lambda e: e.activation(small[:, 2:3], small[:, 1:2], AF.Sqrt), reads=[b("ss1")], writes=[b("ss2")])
                S.op("dve", lambda e: e.reciprocal(small[:, 3:4], small[:, 2:3]), reads=[b("ss2")], writes=[b("rstd")])
                S.op("dve", lambda e, xi=xi: e.scalar_tensor_tensor(xn[:], xt[xi][:], small[:, 3:4], gb[:], ALU.mult, ALU.mult),
                     reads=[b("xt%d" % xi), b("rstd"), b("gb")], writes=[b("xn")])
                for half in range(2):
                    ti = nxt("T", 2)
                    for k in range(8):
                        kc = half * 8 + k
                        S.op("pe", lambda e, ti=ti, k=k, kc=kc: e.transpose(psT[ti][:, k, :], xn[:, kc * 128:(kc + 1) * 128], identb),
                             reads=[b("xn"), b("cbf")], writes=[b("psT%d" % ti)])
                    S.op("act", lambda e, ti=ti, half=half, i=i: e.copy(hT[:, half * 8:(half + 1) * 8, i * 128:(i + 1) * 128], psT[ti][:, :, :]),
                         reads=[b("psT%d" % ti)], writes=[b("hT%d" % i)])

        def proj_tm(i, wv, ncols, bw):
            pi = nxt("P", 2)
            for kc in range(KC):
                S.op("pe", lambda e, pi=pi, kc=kc, i=i: e.matmul(psP[pi][:, 0:ncols], hT[:, kc, i * 128:(i + 1) * 128], wv[:, kc, :],
                                                              start=(kc == 0), stop=(kc == KC - 1)),
                     reads=[b("hT%d" % i), bw], writes=[b("psP%d" % pi)])
            return pi

        def rope(pi, c0, nb, half, cos_ap, sin_ap, dst, bdst, rdeps):
            src = psP[pi][:, c0:c0 + nb * 128].rearrange("p (n c) -> p n c", n=nb)
            x1 = src[:, :, 0:half]
            x2 = src[:, :, half:2 * half]
            cb_ = cos_ap.unsqueeze(1).to_broadcast([128, nb, half])
            sb_ = sin_ap.unsqueeze(1).to_broadcast([128, nb, half])
            n = nb * half
            t = [rp[:, j, 0:n].rearrange("p (n c) -> p n c", n=nb) for j in range(4)]
            bp = b("psP%d" % pi)
            S.op("dve", lambda e: e.tensor_tensor(t[0], x1, cb_, ALU.mult), reads=[bp] + rdeps, writes=[b("rp0")])
            S.op("dve", lambda e: e.tensor_tensor(t[1], x2, sb_, ALU.mult), reads=[bp] + rdeps, writes=[b("rp1")])
            S.op("dve", lambda e: e.tensor_tensor(t[2], x2, cb_, ALU.mult), reads=[bp] + rdeps, writes=[b("rp2")])
            S.op("dve", lambda e: e.tensor_tensor(t[3], x1, sb_, ALU.mult), reads=[bp] + rdeps, writes=[b("rp3")])
            S.op("dve", lambda e: e.tensor_tensor(dst[:, :, 0:half], t[0], t[1], ALU.subtract),
                 reads=[b("rp0"), b("rp1")], writes=bdst)
            S.op("dve", lambda e: e.tensor_tensor(dst[:, :, half:2 * half], t[2], t[3], ALU.add),
                 reads=[b("rp2"), b("rp3")], writes=bdst)

        def load_tB(t65):
            ti = nxt("TB", 2)
            S.dma("sp", tB[ti][:], tabBd[:, t65, :], writes=[b("tB%d" % ti)])
            return ti

        def rope_b(pi, c0, t_i, dst, bdst, scale):
            src = psP[pi][:, c0:c0 + 128]
            x1, x2 = src[:, 0:64], src[:, 64:128]
            cs, sn = tB[t_i][:, 0:64], tB[t_i][:, 64:128]
            bp, bt = b("psP%d" % pi), b("tB%d" % t_i)
            o = rp[:, 4, 0:128]
            t = [rp[:, j, 0:64] for j in range(4)]
            S.op("dve", lambda e: e.tensor_tensor(t[0], x1, cs, ALU.mult), reads=[bp, bt], writes=[b("rp0")])
            S.op("dve", lambda e: e.tensor_tensor(t[1], x2, sn, ALU.mult), reads=[bp, bt], writes=[b("rp1")])
            S.op("dve", lambda e: e.tensor_tensor(t[2], x2, cs, ALU.mult), reads=[bp, bt], writes=[b("rp2")])
            S.op("dve", lambda e: e.tensor_tensor(t[3], x1, sn, ALU.mult), reads=[bp, bt], writes=[b("rp3")])
            S.op("dve", lambda e: e.tensor_tensor(o[:, 0:64], t[0], t[1], ALU.subtract), reads=[b("rp0"), b("rp1")], writes=[b("rp4")])
            S.op("dve", lambda e: e.tensor_tensor(o[:, 64:128], t[2], t[3], ALU.add), reads=[b("rp2"), b("rp3")], writes=[b("rp4")])
            if scale is None:
                S.op("act", lambda e: e.copy(dst, o), reads=[b("rp4")], writes=[bdst])
            else:
                S.op("act", lambda e: e.activation(dst, o, AF.Copy, scale=scale), reads=[b("rp4"), b("ctab")], writes=[bdst])

        def a_kv_part(h, tiles, t33_of, slot_of, kb):
            wv = [wb[kb][:, 0:KC * 384].rearrange("p (k c) -> p k c", k=KC),
                  wb[1 - kb][:, 0:KC * 384].rearrange("p (k c) -> p k c", k=KC)]
            bws = [b("wb%d" % kb), b("wb%d" % (1 - kb))]
            for part, c_base in ((0, AK0), (1, AV0)):
                for g in range(3):
                    load_w(wv[part][:, :, g * 128:(g + 1) * 128], w_in_e[:, c_base + g * 1024 + h * 128: c_base + g * 1024 + (h + 1) * 128], bws[part])
            for i in tiles:
                pi = proj_tm(i, wv[0], 384, bws[0])
                tmv = tm[:, 0:384].rearrange("p (n c) -> p n c", n=3)
                S.op("act", lambda e, pi=pi: e.copy(tm[:, 0:384], psP[pi][:, 0:384]), reads=[b("psP%d" % pi)], writes=TM)
                t33 = t33_of(i)
                rope(pi, 0, 3, 16, col(C_COSA + t33 * 16, 16), col(C_SINA + t33 * 16, 16), tmv, TM, [b("ctab")])
                ti = nxt("T", 2)
                for g in range(3):
                    S.op("pe", lambda e, ti=ti, g=g: e.transpose(psT[ti][:, g, :], tm[:, g * 128:(g + 1) * 128], identb),
                         reads=TM + [b("cbf")], writes=[b("psT%d" % ti)])
                for g in range(3):
                    sl = slot_of(g, i)
                    if sl is None:
                        continue
                    S.op("act", lambda e, ti=ti, g=g, sl=sl: e.copy(KT[:, KT_OFF[g] + sl * 128: KT_OFF[g] + (sl + 1) * 128], psT[ti][:, g, :]),
                         reads=[b("psT%d" % ti)], writes=[b("KT")])
            for i in tiles:
                pi = proj_tm(i, wv[1], 384, bws[1])
                for g in range(3):
                    sl = slot_of(g, i)
                    if sl is None:
                        continue
                    S.op("act", lambda e, pi=pi, g=g, sl=sl: e.copy(Vb[:, V_OFF[g] + sl, 0:128], psP[pi][:, g * 128:(g + 1) * 128]),
                         reads=[b("psP%d" % pi)], writes=[b("Vb")])

        if stage == 0:
            build_hT(xc, 0, 1, g_e)
            a_kv_part(0, range(1), lambda i: i, lambda g, i: 0, 0)
            dbg(0)
        def layer0():
            S.op("dve", lambda e: e.memset(Sf[:, :, :], 0.0), writes=[b("Sf")])
            for p in range(NSB):
                if stage == 2:
                    break
                S.barrier()
                build_hT(xc, p * NPT * 128, NPT, g_e)
                cp(2)
                for h in range(8):
                    wv = wb[h % 2][:, 0:KC * 384].rearrange("p (k c) -> p k c", k=KC)
                    bw = b("wb%d" % (h % 2))
                    load_w(wv[:, :, 0:128], w_in_e[:, BK0 + h * 128:BK0 + (h + 1) * 128], bw)
                    load_w(wv[:, :, 128:384], w_in_e[:, BV0 + h * 256:BV0 + (h + 1) * 256], bw)
                    for i in range(NPT):
                        t48 = p * NPT + i
                        pi = proj_tm(i, wv, 384, bw)
                        t_i = load_tB(t48)
                        rope_b(pi, 0, t_i, kb16[:], b("kb16"), col(C_KDP + t48 * 8 + h))
                        S.op("act", lambda e, pi=pi: e.copy(vb16[:], psP[pi][:, 128:384]), reads=[b("psP%d" % pi)], writes=[b("vb16")])
                        S.op("pe", lambda e, i=i: e.matmul(psO[0][:, 0:256], kb16[:], vb16[:], start=(i == 0), stop=(i == NPT - 1)),
                             reads=[b("kb16"), b("vb16")], writes=[b("psO0")])
                    S.op("dve", lambda e, h=h: e.tensor_tensor(Sf[:, h, :], psO[0][:, 0:256], Sf[:, h, :], ALU.add),
                         reads=[b("psO0"), b("Sf")], writes=[b("Sf")])
                    cp(3)
                cp(3.2 + 0.1 * p)
                if p == NSB - 1:
                    cp(3.5)
                    for h in range(8):
                        a_kv_part(h, range(NPT), lambda i: i,
                                  lambda g, i: (i - (NPT - HALO_T[g])) if i >= NPT - HALO_T[g] else None, h % 2)
                        cp(3.7)
                        for g in range(3):
                            hc = HALO_T[g] * 128
                            ko = (0, 128, 640)[g]
                            S.dma("sp", khalo[h, :, ko:ko + hc], KT[:, KT_OFF[g]:KT_OFF[g] + hc], reads=[b("KT")], writes=[b("khalo")], owner=b("KT"))
                            vo = (0, 1, 5)[g]
                            S.dma("sp", vhalo[h, :, vo:vo + HALO_T[g], :], Vb[:, V_OFF[g]:V_OFF[g] + HALO_T[g], 0:128],
                                  reads=[b("Vb")], writes=[b("vhalo")], owner=b("Vb"))
                        cp(3.8)

            cp(4)
            S.barrier()
            build_hT(xc, NPREV * 128, NT, g_e)
            cp(5)
            inv_sqrt = 128.0 ** -0.5
            for h in range(8):
                for g in range(3):
                    hc = HALO_T[g] * 128
                    ko = (0, 128, 640)[g]
                    vo = (0, 1, 5)[g]
                    S.dma("sp", KT[:, KT_OFF[g]:KT_OFF[g] + hc], khalo[h, :, ko:ko + hc], reads=[b("khalo")], writes=[b("KT")])
                    S.dma("sp", Vb[:, V_OFF[g]:V_OFF[g] + HALO_T[g], 0:128], vhalo[h, :, vo:vo + HALO_T[g], :], reads=[b("vhalo")], writes=[b("Vb")])
                a_kv_part(h, range(NT), lambda i: 16 + i, lambda g, i: HALO_T[g] + i, h % 2)
                wq = wb[h % 2][:, 0:KC * 512].rearrange("p (k c) -> p k c", k=KC)
                bwq = b("wb%d" % (h % 2))
                for g in range(3):
                    load_w(wq[:, :, g * 128:(g + 1) * 128], w_in_e[:, AQ0 + g * 1024 + h * 128:AQ0 + g * 1024 + (h + 1) * 128], bwq)
                load_w(wq[:, :, 384:512], w_in_e[:, G0 + h * 128:G0 + (h + 1) * 128], bwq)
                for i in range(NT):
                    pi = proj_tm(i, wq, 512, bwq)
                    tmv = tm[:, 0:384].rearrange("p (n c) -> p n c", n=3)
                    S.op("act", lambda e, pi=pi: e.copy(tm[:, 0:384], psP[pi][:, 0:384]), reads=[b("psP%d" % pi)], writes=TM)
                    S.op("act", lambda e, pi=pi: e.activation(sg[:, 0:128], psP[pi][:, 384:512], AF.Silu), reads=[b("psP%d" % pi)], writes=[b("sg")])
                    t33 = 16 + i
                    rope(pi, 0, 3, 16, col(C_COSA + t33 * 16, 16), col(C_SINA + t33 * 16, 16), tmv, TM, [b("ctab")])
                    ti = nxt("T", 2)
                    for g in range(3):
                        S.op("pe", lambda e, ti=ti, g=g: e.transpose(psT[ti][:, g, :], tm[:, g * 128:(g + 1) * 128], identb),
                             reads=TM + [b("cbf")], writes=[b("psT%d" % ti)])
                    S.op("act", lambda e, ti=ti: e.copy(QT[:, 0:3, :], psT[ti][:, 0:3, :]), reads=[b("psT%d" % ti)], writes=[b("QT")])
                    oi = nxt("O", 2)
                    for ch in range(6):
                        si = nxt("S", 2)
                        pti = nxt("PT", 3)
                        for j in range(4):
                            g, o = MI[ch * 4 + j]
                            sl = HALO_T[g] + i - o
                            S.op("pe", lambda e, si=si, j=j, g=g, sl=sl: e.matmul(
                                psS[si][:, j * 128:(j + 1) * 128], KT[:, KT_OFF[g] + sl * 128:KT_OFF[g] + (sl + 1) * 128], QT[:, g, :],
                                start=True, stop=True), reads=[b("KT"), b("QT")], writes=[b("psS%d" % si)])
                        S.op("act", lambda e, si=si, pti=pti: e.activation(PT[pti][:], psS[si][:], AF.Exp, scale=inv_sqrt),
                             reads=[b("psS%d" % si)], writes=[b("PT%d" % pti)])
                        S.op("dve", lambda e, pti=pti, ch=ch: e.tensor_tensor(PT[pti][:], PT[pti][:], cbf[:, 128 + ch * 512:128 + (ch + 1) * 512], ALU.mult),
                             reads=[b("PT%d" % pti), b("cbf")], writes=[b("PT%d" % pti)])
                        for j in range(4):
                            g, o = MI[ch * 4 + j]
                            sl = HALO_T[g] + i - o
                            S.op("pe", lambda e, oi=oi, pti=pti, j=j, g=g, sl=sl, ch=ch: e.matmul(
                                psO[oi][:, 0:129], PT[pti][:, j * 128:(j + 1) * 128], Vb[:, V_OFF[g] + sl, 0:129],
                                start=(ch == 0 and j == 0), stop=(ch == 5 and j == 3)),
                                reads=[b("PT%d" % pti), b("Vb")], writes=[b("psO%d" % oi)])
                    S.op("dve", lambda e, oi=oi: e.tensor_scalar_max(small[:, 8:9], psO[oi][:, 128:129], 1e-30),
                         reads=[b("psO%d" % oi)], writes=[b("den")])
                    S.op("dve", lambda e: e.reciprocal(small[:, 9:10], small[:, 8:9]), reads=[b("den")], writes=[b("rden")])
                    S.op("dve", lambda e, oi=oi: e.scalar_tensor_tensor(ytm[:, 0:128], psO[oi][:, 0:128], small[:, 9:10], sg[:, 0:128], ALU.mult, ALU.mult),
                         reads=[b("psO%d" % oi), b("rden"), b("sg")], writes=[b("ytm")])
                    ti = nxt("T", 2)
                    S.op("pe", lambda e, ti=ti: e.transpose(psT[ti][:, 0, :], ytm[:, 0:128], identb), reads=[b("ytm"), b("cbf")], writes=[b("psT%d" % ti)])
                    S.op("act", lambda e, ti=ti, i=i: e.copy(yTu[:, 0, i * 128:(i + 1) * 128], psT[ti][:, 0, :]), reads=[b("psT%d" % ti)], writes=[b("yTu")])
                S.dma("sp", yTs[h, :, :], yTu[:, 0, :], reads=[b("yTu")], writes=[b("yTs")], owner=b("yTu"))
                cp(6)
            cp(7)

            for h in range(8):
                if h % 2 == 0:
                    wq = wb[0][:, 0:KC * 512].rearrange("p (k c) -> p k c", k=KC)
                    wg = wb[1][:, 0:KC * 256].rearrange("p (k c) -> p k c", k=KC)
                    bwq, bwg = b("wb0"), b("wb1")
                else:
                    wq = UNI[:, 0:KC * 512].rearrange("p (k c) -> p k c", k=KC)
                    wg = UNI[:, 9216:9216 + KC * 256].rearrange("p (k c) -> p k c", k=KC)
                    bwq, bwg = b("KT"), b("Vb")
                load_w(wq[:, :, 0:128], w_in_e[:, BQ0 + h * 128:BQ0 + (h + 1) * 128], bwq)
                load_w(wq[:, :, 128:256], w_in_e[:, BK0 + h * 128:BK0 + (h + 1) * 128], bwq)
                load_w(wq[:, :, 256:512], w_in_e[:, BV0 + h * 256:BV0 + (h + 1) * 256], bwq)
                load_w(wg, w_in_e[:, G0 + 1024 + h * 256:G0 + 1024 + (h + 1) * 256], bwg)
                S.op("act", lambda e, h=h: e.copy(Sb16[:], Sf[:, h, :]), reads=[b("Sf")], writes=[b("Sb16")])
                cg = GAM[h] ** 128
                for i in range(NT):
                    pi = proj_tm(i, wq, 512, bwq)
                    t_i = load_tB(NPREV + i)
                    rope_b(pi, 0, t_i, tm[:, 0:128], b("tm0"), None)
                    rope_b(pi, 128, t_i, tm[:, 128:256], b("tm1"), col(C_KDO + h))
                    S.op("act", lambda e: e.copy(kb16[:], tm[:, 128:256]), reads=[b("tm1")], writes=[b("kb16")])
                    S.op("act", lambda e, pi=pi: e.copy(vb16[:], psP[pi][:, 256:512]), reads=[b("psP%d" % pi)], writes=[b("vb16")])
                    pg = proj_tm(i, wg, 256, bwg)
                    S.op("act", lambda e, pg=pg: e.activation(sg[:, 0:256], psP[pg][:, 0:256], AF.Silu), reads=[b("psP%d" % pg)], writes=[b("sg")])
                    ti = nxt("T", 2)
                    S.op("pe", lambda e, ti=ti: e.transpose(psT[ti][:, 0, :], tm[:, 0:128], identb), reads=[b("tm0"), b("cbf")], writes=[b("psT%d" % ti)])
                    S.op("pe", lambda e, ti=ti: e.transpose(psT[ti][:, 1, :], tm[:, 128:256], identb), reads=[b("tm1"), b("cbf")], writes=[b("psT%d" % ti)])
                    S.op("act", lambda e, ti=ti: e.copy(QT[:, 0:2, :], psT[ti][:, 0:2, :]), reads=[b("psT%d" % ti)], writes=[b("QT")])
                    S.op("pe", lambda e: e.matmul(psS[0][:, 0:128], QT[:, 1, :], QT[:, 0, :], start=True, stop=True),
                         reads=[b("QT")], writes=[b("psS0")])
                    S.op("dve", lambda e: e.tensor_tensor(PT[0][:, 0:128], psS[0][:, 0:128], cbf[:, 128:256], ALU.mult),
                         reads=[b("psS0"), b("cbf")], writes=[b("PT0")])
                    oi = nxt("O", 2)
                    S.op("pe", lambda e, oi=oi: e.matmul(psO[oi][:, 0:256], PT[0][:, 0:128], vb16[:], start=True, stop=False),
                         reads=[b("PT0"), b("vb16")], writes=[b("psO%d" % oi)])
                    S.op("pe", lambda e, oi=oi: e.matmul(psO[oi][:, 0:256], QT[:, 0, :], Sb16[:], start=False, stop=True),
                         reads=[b("QT"), b("Sb16")], writes=[b("psO%d" % oi)])
                    S.op("pe", lambda e: e.matmul(psS[1][:, 0:256], kb16[:], vb16[:], start=True, stop=True),
                         reads=[b("kb16"), b("vb16")], writes=[b("psS1")])
                    S.op("act", lambda e, oi=oi: e.activation(xn[:, 0:256], psO[oi][:, 0:256], AF.Square, accum_out=small[:, 12:13]),
                         reads=[b("psO%d" % oi)], writes=[b("xn"), b("zss")])
                    S.op("dve", lambda e, h=h: e.tensor_scalar(small[:, 13:14], small[:, 12:13], 1.0 / 256, col(C_EPSB + h), ALU.mult, ALU.add),
                         reads=[b("zss"), b("ctab")], writes=[b("zs1")])
                    S.op("act", lambda e: e.activation(small[:, 14:15], small[:, 13:14], AF.Sqrt), reads=[b("zs1")], writes=[b("zs2")])
                    S.op("dve", lambda e: e.reciprocal(small[:, 15:16], small[:, 14:15]), reads=[b("zs2")], writes=[b("zr")])
                    S.op("dve", lambda e, oi=oi: e.scalar_tensor_tensor(ytm[:, 0:256], psO[oi][:, 0:256], small[:, 15:16], sg[:, 0:256], ALU.mult, ALU.mult),
                         reads=[b("psO%d" % oi), b("zr"), b("sg")], writes=[b("ytm")])
                    ti = nxt("T", 2)
                    for e2 in range(2):
                        S.op("pe", lambda e, ti=ti, e2=e2: e.transpose(psT[ti][:, e2, :], ytm[:, e2 * 128:(e2 + 1) * 128], identb),
                             reads=[b("ytm"), b("cbf")], writes=[b("psT%d" % ti)])
                    S.op("act", lambda e, ti=ti, i=i: e.copy(yTu[:, :, i * 128:(i + 1) * 128], psT[ti][:, 0:2, :]), reads=[b("psT%d" % ti)], writes=[b("yTu")])
                    S.op("dve", lambda e, h=h: e.tensor_tensor(Sf[:, h, :], psS[1][:, 0:256], Sf[:, h, :], ALU.add),
                         reads=[b("psS1"), b("Sf")], writes=[b("Sf")])
                    S.op("dve", lambda e, h=h, cg=cg: e.tensor_scalar_mul(Sf[:, h, :], Sf[:, h, :], cg), reads=[b("Sf")], writes=[b("Sf")])
                    S.op("act", lambda e, h=h: e.copy(Sb16[:], Sf[:, h, :]), reads=[b("Sf")], writes=[b("Sb16")])
                for e2 in range(2):
                    S.dma("sp", yTs[8 + 2 * h + e2, :, :], yTu[:, e2, :], reads=[b("yTu")], writes=[b("yTs")], owner=b("yTu"))

            cp(8)
            S.barrier()
            for (t0, t1) in ((0, 9), (9, 17)):
                S.barrier()
                ntk = (t1 - t0) * 128
                yv = BIG[:, 0:24 * ntk].rearrange("p (k t) -> p k t", k=24)
                for k in range(24):
                    S.dma("sp", yv[:, k, :], yTs[k, :, t0 * 128:t1 * 128], reads=[b("yTs")], writes=[b("yv")])
                for cbk in range(8):
                    wi = nxt("W", 2)
                    wv = wb[wi][:, 0:24 * 256].rearrange("p (k c) -> p k c", k=24)
                    load_w(wv, w_out_e[:, cbk * 256:(cbk + 1) * 256], b("wb%d" % wi))
                    for i in range(t0, t1):
                        xi = nxt("X", 2)
                        S.dma("sp", xt[xi][:, 0:256], xc[(NPREV + i) * 128:(NPREV + i + 1) * 128, cbk * 256:(cbk + 1) * 256], writes=[b("xt%d" % xi)])
                        pi = nxt("P", 2)
                        for k in range(24):
                            S.op("pe", lambda e, pi=pi, k=k, i=i, wv=wv, yv=yv, t0=t0: e.matmul(psP[pi][:, 0:256], yv[:, k, (i - t0) * 128:(i - t0 + 1) * 128], wv[:, k, :],
                                                                          start=(k == 0), stop=(k == 23)),
                                 reads=[b("yv"), b("wb%d" % wi)], writes=[b("psP%d" % pi)])
                        S.op("dve", lambda e, pi=pi, xi=xi: e.tensor_tensor(xt[xi][:, 256:512], psP[pi][:, 0:256], xt[xi][:, 0:256], ALU.add),
                             reads=[b("psP%d" % pi), b("xt%d" % xi)], writes=[b("xo%d" % xi)])
                        S.dma("sp", x1s[i * 128:(i + 1) * 128, cbk * 256:(cbk + 1) * 256], xt[xi][:, 256:512],
                              reads=[b("xo%d" % xi)], writes=[b("x1s")], owner=b("xo%d" % xi))
            S.barrier()
        if stage != 5:
            layer0()
        if stage < 5:
            S.finish()
            return nc

        ubuf = UNI[:, 0:2176]
        dg = UNI[:, 2176:6144].rearrange("p (k c) -> p k c", k=31)
        sgbuf = UNI[:, 6144:8192]
        vbuf = UNI[:, 8192:12288].bitcast(F32)
        sig = UNI[:, 12288:13312].bitcast(F32)
        Abuf = UNI[:, 13312:17408].bitcast(F32)
        Bbuf = UNI[:, 17408:21504].bitcast(F32)
        vb2 = UNI[:, 21504:22016]
        vsq = UNI[:, 22016:22528]
        c1 = UNI[:, 22528:24896].bitcast(F32)
        onesb = UNI[:, 24896:24898]
        ybuf = UNI[:, 24960:27008]
        onesf = ctab[:, C_ONE:C_ONE + 128]
        S.barrier()
        S.dma("sp", c1, c1d[:, :], writes=[b("c1")])
        S.op("dve", lambda e: e.memset(onesb, 1.0), writes=[b("onesb")])
        S.op("dve", lambda e: e.memset(PT[0][:, :], 0.0), writes=[b("PT0")])
        build_hT(x1s, 0, NT, g_o)
        S.op("pe", lambda e: e.matmul(psO[0][:, 0:32], PT[0][:, 0:128], PT[0][:, 0:32], start=True, stop=False, skip_group_check=True),
             reads=[b("PT0")], writes=[b("psO0")])

        def c1c(c):
            return c1[:, c:c + 1]

        groups = [(0, 128)] + [(128 + m * 512, 512) for m in range(4)]
        for j in range(32):
            wi = nxt("W", 2)
            bw = b("wb%d" % wi)
            wv = wb[wi][:, 0:KC * 384].rearrange("p (k c) -> p k c", k=KC)
            for part in range(3):
                load_w(wv[:, :, part * 128:(part + 1) * 128], w_in_o[:, part * 4096 + j * 128:part * 4096 + (j + 1) * 128], bw)
            for k in range(31):
                S.op("pool", lambda e, k=k, j=j: e.tensor_scalar_mul(dg[:, k, :], identb, c1c(192 + j * 31 + k)),
                     reads=[b("cbf"), b("c1")], writes=[b("dg")])
            for gi, (t0, n) in enumerate(groups):
                tl = list(range(t0 // 128, (t0 + n) // 128))
                parts = ((0, psP[0], "psP0"), (1, psP[1], "psP1")) + (((2, psS[0], "psS0"),) if gi > 0 else ())
                for part, pdst, pn in parts:
                    for kc in range(KC):
                        S.op("pe", lambda e, pdst=pdst, part=part, kc=kc, t0=t0, n=n, wv=wv: e.matmul(
                            pdst[:, 0:n], wv[:, kc, part * 128:(part + 1) * 128], hT[:, kc, t0:t0 + n], start=(kc == 0), stop=(kc == KC - 1)),
                            reads=[b("hT%d" % t) for t in tl] + [bw], writes=[b(pn)])
                S.op("act", lambda e, n=n, j=j: e.activation(sig[:, 0:n], psP[1][:, 0:n], AF.Sigmoid, bias=c1c(32 + j)),
                     reads=[b("psP1"), b("c1")], writes=[b("sig")])
                S.op("dve", lambda e, n=n, j=j, t0=t0: e.scalar_tensor_tensor(ubuf[:, t0:t0 + n], psP[0][:, 0:n], c1c(j), sig[:, 0:n], ALU.add, ALU.mult),
                     reads=[b("psP0"), b("sig"), b("c1")], writes=[b("ubuf")])
                if gi == 0:
                    S.op("dve", lambda e: e.tensor_scalar_mul(ubuf[:, 0:128], ubuf[:, 0:128], col(C_UV)),
                         reads=[b("ubuf"), b("ctab")], writes=[b("ubuf")])
                else:
                    S.op("act", lambda e, n=n, j=j, t0=t0: e.activation(sgbuf[:, t0 - 128:t0 - 128 + n], psS[0][:, 0:n], AF.Silu, bias=c1c(64 + j)),
                         reads=[b("psS0"), b("c1")], writes=[b("sgbuf")])
            for m in range(4):
                for k in range(31):
                    S.op("pe", lambda e, k=k, m=m: e.matmul(psS[1][:, 0:512], dg[:, k, :], ubuf[:, 98 + m * 512 + k:98 + m * 512 + k + 512],
                                                       start=(k == 0), stop=(k == 30)),
                         reads=[b("dg"), b("ubuf")], writes=[b("psS1")])
                S.op("act", lambda e, m=m, j=j: e.activation(vbuf[:, m * 512:(m + 1) * 512], psS[1][:, 0:512], AF.Identity, bias=c1c(96 + j)),
                     reads=[b("psS1"), b("c1")], writes=[b("vbuf")])
                S.op("act", lambda e, m=m: e.copy(vb2, vbuf[:, m * 512:(m + 1) * 512]), reads=[b("vbuf")], writes=[b("vb2")])
                S.op("act", lambda e, m=m: e.activation(vsq, vbuf[:, m * 512:(m + 1) * 512], AF.Square), reads=[b("vbuf")], writes=[b("vsq")])
                for t in range(4):
                    tile_ = m * 4 + t
                    last = (j == 31)
                    S.op("pe", lambda e, t=t, tile_=tile_, last=last: e.matmul(psO[0][:, 2 * tile_:2 * tile_ + 1], vb2[:, t * 128:(t + 1) * 128], onesb[:, 0:1],
                                                                       start=False, stop=last, skip_group_check=True),
                         reads=[b("vb2"), b("onesb")], writes=[b("psO0")])
                    S.op("pe", lambda e, t=t, tile_=tile_, last=last: e.matmul(psO[0][:, 2 * tile_ + 1:2 * tile_ + 2], vsq[:, t * 128:(t + 1) * 128], onesb[:, 0:1],
                                                                       start=False, stop=last, skip_group_check=True),
                         reads=[b("vsq"), b("onesb")], writes=[b("psO0")])
            S.dma("sp", vS[j, :, :], vbuf, reads=[b("vbuf")], writes=[b("vS")], owner=b("vbuf"))
            S.dma("sp", sgS[j, :, :], sgbuf, reads=[b("sgbuf")], writes=[b("sgS")], owner=b("sgbuf"))

        st1 = rp[:, 2, :]
        S.op("dve", lambda e: e.tensor_scalar_mul(st1[:, 0:32], psO[0][:, 0:32], 1.0 / 4096), reads=[b("psO0")], writes=[b("st1")])
        mean = st1[:, 0:32].rearrange("p (t s) -> p t s", s=2)[:, :, 0:1]
        ex2 = st1[:, 0:32].rearrange("p (t s) -> p t s", s=2)[:, :, 1:2]

        def sv(c0):
            return st1[:, c0:c0 + 16].unsqueeze(2)
        S.op("dve", lambda e: e.tensor_tensor(sv(32), mean, mean, ALU.mult), reads=[b("st1")], writes=[b("st1")])
        S.op("dve", lambda e: e.tensor_tensor(sv(48), ex2, sv(32), ALU.subtract), reads=[b("st1")], writes=[b("st1")])
        S.op("dve", lambda e: e.tensor_scalar_add(st1[:, 48:64], st1[:, 48:64], EPS), reads=[b("st1")], writes=[b("st1")])
        S.op("act", lambda e: e.activation(st1[:, 48:64], st1[:, 48:64], AF.Sqrt), reads=[b("st1")], writes=[b("st1")])
        S.op("dve", lambda e: e.reciprocal(st1[:, 64:80], st1[:, 48:64]), reads=[b("st1")], writes=[b("st1")])
        S.op("dve", lambda e: e.scalar_tensor_tensor(sv(80), mean, -1.0, sv(64), ALU.mult, ALU.mult), reads=[b("st1")], writes=[b("st1")])
        for t in range(16):
            S.op("dve", lambda e, t=t: e.tensor_scalar_mul(rp[:, 0, :], identf, st1[:, 64 + t:65 + t]), reads=[b("st1"), b("ctab")], writes=[b("rp0")])
            S.op("dve", lambda e, t=t: e.tensor_scalar_mul(rp[:, 1, :], identf, st1[:, 80 + t:81 + t]), reads=[b("st1"), b("ctab")], writes=[b("rp1")])
            S.op("pe", lambda e: e.matmul(psS[0][:, 0:128], onesf, rp[:, 0, :], start=True, stop=True), reads=[b("ctab"), b("rp0")], writes=[b("psS0")])
            S.op("pe", lambda e: e.matmul(psS[0][:, 128:256], onesf, rp[:, 1, :], start=True, stop=True), reads=[b("ctab"), b("rp1")], writes=[b("psS0")])
            S.op("act", lambda e, t=t: e.copy(Abuf[:, t * 128:(t + 1) * 128], psS[0][:, 0:128]), reads=[b("psS0")], writes=[b("Abuf")])
            S.op("act", lambda e, t=t: e.copy(Bbuf[:, t * 128:(t + 1) * 128], psS[0][:, 128:256]), reads=[b("psS0")], writes=[b("Bbuf")])
        for j in range(32):
            S.dma("sp", vbuf, vS[j, :, :], reads=[b("vS")], writes=[b("vbuf")])
            S.dma("sp", sgbuf, sgS[j, :, :], reads=[b("sgS")], writes=[b("sgbuf")])
            S.op("dve", lambda e: e.tensor_tensor(vbuf, vbuf, Abuf, ALU.mult), reads=[b("vbuf"), b("Abuf")], writes=[b("vbuf")])
            S.op("pool", lambda e: e.tensor_tensor(vbuf, vbuf, Bbuf, ALU.add), reads=[b("vbuf"), b("Bbuf")], writes=[b("vbuf")])
            S.op("act", lambda e, j=j: e.activation(vbuf, vbuf, AF.Silu, bias=c1c(160 + j), scale=c1c(128 + j)),
                 reads=[b("vbuf"), b("c1")], writes=[b("vbuf")])
            S.op("dve", lambda e: e.tensor_tensor(ybuf, vbuf, sgbuf, ALU.mult), reads=[b("vbuf"), b("sgbuf")], writes=[b("ybuf")])
            S.dma("sp", yT1s[j, :, :], ybuf, reads=[b("ybuf")], writes=[b("yT1s")], owner=b("ybuf"))

        S.barrier()
        S.dma("sp", gb[:], b_o.partition_broadcast(128), writes=[b("gb")])
        for tg in range(2):
            S.barrier()
            yv = BIG[:, 0:32 * 1024].rearrange("p (k t) -> p k t", k=32)
            for k in range(32):
                S.dma("sp", yv[:, k, :], yT1s[k, :, tg * 1024:(tg + 1) * 1024], reads=[b("yT1s")], writes=[b("yv")])
            for cbk in range(8):
                wi = nxt("W", 2)
                wv = wb[wi][:, 0:32 * 256].rearrange("p (k c) -> p k c", k=32)
                load_w(wv, w_out_o[:, cbk * 256:(cbk + 1) * 256], b("wb%d" % wi))
                for i8 in range(8):
                    tile_ = tg * 8 + i8
                    xi = nxt("X", 2)
                    S.dma("sp", xt[xi][:, 0:256], x1s[(tile_ + 1) * 128:(tile_ + 2) * 128, cbk * 256:(cbk + 1) * 256],
                          reads=[b("x1s")], writes=[b("xt%d" % xi)])
                    pi = nxt("P", 2)
                    for k in range(32):
                        S.op("pe", lambda e, pi=pi, k=k, i8=i8, wv=wv, yv=yv: e.matmul(psP[pi][:, 0:256], yv[:, k, i8 * 128:(i8 + 1) * 128], wv[:, k, :],
                                                                           start=(k == 0), stop=(k == 31)),
                             reads=[b("yv"), b("wb%d" % wi)], writes=[b("psP%d" % pi)])
                    S.op("dve", lambda e, pi=pi, xi=xi: e.tensor_tensor(xt[xi][:, 256:512], psP[pi][:, 0:256], xt[xi][:, 0:256], ALU.add),
                         reads=[b("psP%d" % pi), b("xt%d" % xi)], writes=[b("xo%d" % xi)])
                    S.op("dve", lambda e, xi=xi, cbk=cbk: e.tensor_tensor(xt[xi][:, 256:512], xt[xi][:, 256:512], gb[:, cbk * 256:(cbk + 1) * 256], ALU.add),
                         reads=[b("xo%d" % xi), b("gb")], writes=[b("xo%d" % xi)])
                    S.dma("sp", x2s[tile_ * 128:(tile_ + 1) * 128, cbk * 256:(cbk + 1) * 256], xt[xi][:, 256:512],
                          reads=[b("xo%d" % xi)], writes=[b("x2s")], owner=b("xo%d" % xi))

        S.barrier()
        S.dma("sp", gb[:], g_f.partition_broadcast(128), writes=[b("gb")])
        for tile_ in range(16):
            xi = nxt("X", 2)
            S.dma("sp", xt[xi][:], x2s[tile_ * 128:(tile_ + 1) * 128, :], reads=[b("x2s")], writes=[b("xt%d" % xi), b("xo%d" % xi)])
            S.op("act", lambda e, xi=xi: e.activation(xn[:], xt[xi][:], AF.Square, accum_out=small[:, 0:1]),
                 reads=[b("xt%d" % xi)], writes=[b("xn"), b("ss")])
            S.op("dve", lambda e: e.tensor_scalar(small[:, 1:2], small[:, 0:1], 1.0 / D, EPS, ALU.mult, ALU.add), reads=[b("ss")], writes=[b("ss1")])
            S.op("act", lambda e: e.activation(small[:, 2:3], small[:, 1:2], AF.Sqrt), reads=[b("ss1")], writes=[b("ss2")])
            S.op("dve", lambda e: e.reciprocal(small[:, 3:4], small[:, 2:3]), reads=[b("ss2")], writes=[b("rstd")])
            S.op("dve", lambda e, xi=xi: e.scalar_tensor_tensor(xt[xi][:], xt[xi][:], small[:, 3:4], gb[:], ALU.mult, ALU.mult),
                 reads=[b("xt%d" % xi), b("rstd"), b("gb")], writes=[b("xt%d" % xi)])
            S.dma("sp", outd[tile_ * 128:(tile_ + 1) * 128, :], xt[xi][:], reads=[b("xt%d" % xi)], writes=[b("out")], owner=b("xt%d" % xi))
        S.finish()
    return nc


def _f32(a):
    return np.ascontiguousarray(a, dtype=np.float32)


def make_consts(q):
    T0 = 2048 * q
    p = np.arange(128)
    ct = np.zeros((128, NCT), np.float32)
    invA = (np.float32(500000.0) ** (-np.arange(0, 32, 2, dtype=np.float32) / np.float32(32))).astype(np.float32)
    for t in range(33):
        pos = (T0 - 128 - 2048 + t * 128 + p).astype(np.float32)
        ang = pos[:, None] * invA[None, :]
        ct[:, C_COSA + t * 16:C_COSA + (t + 1) * 16] = np.cos(ang)
        ct[:, C_SINA + t * 16:C_SINA + (t + 1) * 16] = np.sin(ang)
        ct[:, C_VAL + t] = 1.0 if (T0 - 128 - 2048 + t * 128) >= 0 else 0.0
    logg = np.log1p(-np.power(2.0, -5.0 - np.arange(8, dtype=np.float64)))
    for t in range(NPREV):
        dist = (NPREV * 128 - 1) - (t * 128 + p)
        ct[:, C_KDP + t * 8:C_KDP + (t + 1) * 8] = np.exp(dist[:, None] * logg[None, :]) * (128.0 ** -0.5)
    ct[:, C_KDO:C_KDO + 8] = np.exp(-(p[:, None] + 1.0) * logg[None, :]) * (128.0 ** -0.5)
    ct[:, C_EPSB:C_EPSB + 8] = EPS * np.exp(-2.0 * (p[:, None] + 1.0) * logg[None, :])
    ct[:, C_UV] = 1.0 if q > 0 else 0.0
    ct[:, C_IDF:C_IDF + 128] = np.eye(128, dtype=np.float32)
    ct[:, C_ONE:C_ONE + 128] = 1.0
    invB = (np.float32(10000.0) ** (-np.arange(0, 128, 2, dtype=np.float32) / np.float32(128))).astype(np.float32)
    tb = np.zeros((128, NXT, 128), np.float32)
    for t in range(NXT):
        pos = (T0 - 128 - NPREV * 128 + t * 128 + p).astype(np.float32)
        ang = pos[:, None] * invB[None, :]
        tb[:, t, 0:64] = np.cos(ang)
        tb[:, t, 64:128] = np.sin(ang)
    cb = np.zeros((128, 3200), np.float32)
    cb[:, 0:128] = np.eye(128)
    kk = p[:, None]
    qq = p[None, :]
    for m, (g, o) in enumerate(MI):
        d = DIL[g]
        delta = 128 * o + qq - kk
        cb[:, 128 + m * 128:128 + (m + 1) * 128] = ((delta >= 0) & (delta % d == 0) & (delta <= 128 * d)).astype(np.float32)
    return ct, tb, cb.astype(ml_dtypes.bfloat16)


_NC_CACHE = {}


def make_in_maps(inputs):
    x = _f32(inputs["x"])
    shared = {
        "w_in_e": _f32(inputs["w_in_even"][0]), "w_out_e": _f32(inputs["w_out_even"][0]),
        "w_in_o": _f32(inputs["w_in_odd"][0]), "w_out_o": _f32(inputs["w_out_odd"][0]),
        "g_e": _f32(inputs["norm_even"][0]), "g_o": _f32(inputs["norm_odd"][0]),
        "g_f": _f32(inputs["final_norm"]), "b_o": _f32(inputs["b_out_odd"][0]),
    }
    def pj(v):
        return _f32(v).reshape(32, 128).T
    b_in = _f32(inputs["b_in_odd"][0])
    cw = _f32(inputs["conv_w_odd"][0])
    c1 = np.zeros((128, 32 * 37), np.float32)
    for idx, v in enumerate((b_in[0:4096], b_in[4096:8192], b_in[8192:12288], inputs["conv_b_odd"][0],
                             inputs["ln_g_odd"][0], inputs["ln_b_odd"][0])):
        c1[:, idx * 32:(idx + 1) * 32] = pj(v)
    c1[:, 192:] = cw.reshape(31, 32, 128).transpose(2, 1, 0).reshape(128, 32 * 31)
    shared["c1"] = c1
    maps = []
    for c in range(8):
        bi, q = c // 4, c % 4
        T0 = 2048 * q
        xcore = np.zeros((NXT * 128, D), np.float32)
        lo = T0 - 128 - NPREV * 128
        s0 = max(lo, 0)
        xcore[s0 - lo:] = x[bi, s0:T0 + 2048]
        ct, tb, cb = make_consts(q)
        m = dict(shared)
        m.update({"xc": xcore, "ctab": ct, "tabB": tb, "cbf": cb})
        maps.append(m)
    return maps


def kernel(**inputs):
    if "nc" not in _NC_CACHE:
        _NC_CACHE["nc"] = build(9)
    nc = _NC_CACHE["nc"]
    maps = make_in_maps(inputs)
    res = run_bass_kernel_spmd(nc, maps, core_ids=list(range(8)))
    out = np.zeros((2, 8192, D), np.float32)
    for c in range(8):
        bi, q = c // 4, c % 4
        out[bi, q * 2048:(q + 1) * 2048] = res.results[c]["out"]
    return out
```

```python
import numpy as np
import ml_dtypes
from contextlib import ExitStack
import concourse.bass as bass
import concourse.mybir as mybir
from concourse.bass_utils import run_bass_kernel_spmd

F32 = mybir.dt.float32
BF16 = mybir.dt.bfloat16
AF = mybir.ActivationFunctionType
ALU = mybir.AluOpType

D = 2048
KC = 16
NT = 17
NPT = 16
NSB = 3
NPREV = NSB * NPT
NXT = NPREV + NT
TOK = NT * 128
EPS = 1e-6
AQ0, AK0, AV0, BQ0, BK0, BV0, G0 = 0, 3072, 6144, 9216, 10240, 11264, 13312
DIL = (1, 4, 16)
HALO_T = (1, 4, 16)
KT_OFF = (0, 2304, 4992)
V_OFF = (0, 18, 39)
MI = [(g, o) for g in range(3) for o in range(HALO_T[g] + 1)]
C_COSA, C_SINA, C_KDP, C_KDO, C_EPSB, C_VAL, C_UV, C_IDF, C_ONE = 0, 528, 1056, 1440, 1448, 1456, 1489, 1490, 1618
NCT = 1619 + 127
GAM = [1.0 - 2.0 ** (-5 - h) for h in range(8)]


class Buf:
    __slots__ = ("name", "w", "r", "sem", "semv")

    def __init__(self, name):
        self.name = name
        self.w = None
        self.r = {}
        self.sem = None
        self.semv = 0


class Sched:
    ENG = ("pe", "dve", "act", "pool", "sp")

    def __init__(self, nc, stack):
        self.nc = nc
        self.stack = stack
        self.prog = {e: [] for e in self.ENG}
        self.cnt = {e: 0 for e in self.ENG}
        self.sem = {e: stack.enter_context(nc.semaphore("s_" + e)) for e in self.ENG}
        self.seen = {e: {} for e in self.ENG}
        self.dmabufs = []

    def _wait(self, e, tok):
        sem, v, key, src = tok
        if src == e and e in ("pe", "sp"):
            return
        if self.seen[e].get(key, 0) >= v:
            return
        self.seen[e][key] = v
        self.prog[e].append(lambda eng, sem=sem, v=v: eng.wait_ge(sem, v))

    def _deps(self, e, reads, writes):
        for b in reads:
            if b.w is not None:
                self._wait(e, b.w)
        for b in writes:
            if b.w is not None:
                self._wait(e, b.w)
            for t in b.r.values():
                self._wait(e, t)

    def _mark(self, tok, reads, writes):
        for b in writes:
            b.w = tok
            b.r = {}
        for b in reads:
            b.r[tok[2]] = tok

    def op(self, e, fn, reads=(), writes=()):
        psr = [x for x in reads if x.name.startswith("ps")]
        if psr:
            reads = [x for x in reads if not x.name.startswith("ps")]
            writes = list(writes) + psr
        self._deps(e, reads, writes)
        self.cnt[e] += 1
        sem = self.sem[e]
        self.prog[e].append(lambda eng, fn=fn, sem=sem: fn(eng).then_inc(sem, 1))
        self._mark((sem, self.cnt[e], e, e), reads, writes)

    def dma(self, q, out, in_, reads=(), writes=(), owner=None):
        self._deps(q, reads, writes)
        if owner is None:
            owner = writes[0] if writes else reads[0]
        if owner.sem is None:
            owner.sem = self.stack.enter_context(self.nc.semaphore("d_" + owner.name))
            self.dmabufs.append(owner)
        owner.semv += 16
        sem = owner.sem
        self.prog[q].append(lambda eng, out=out, in_=in_, sem=sem: eng.dma_start(out=out, in_=in_).then_inc(sem, 16))
        self._mark((sem, owner.semv, "d_" + owner.name, "dma"), reads, writes)

    def barrier(self):
        for e in self.ENG:
            for f in self.ENG:
                if f != e and self.cnt[f] > 0:
                    self._wait(e, (self.sem[f], self.cnt[f], f, f))
            for b in self.dmabufs:
                self._wait(e, (b.sem, b.semv, "d_" + b.name, "dma"))

    def finish(self):
        self.barrier()
        with self.nc.Block() as block:
            for e, deco in (("pe", block.tensor), ("dve", block.vector), ("act", block.scalar),
                            ("pool", block.gpsimd), ("sp", block.sync)):
                def body(eng, prog=self.prog[e]):
                    for f in prog:
                        f(eng)
                deco(body)


class StopBuild(Exception):
    pass


def build(stage=9, cp_stop=None):
    try:
        return _build(stage, cp_stop)
    except StopBuild as e:
        return e.args[0]


def _build(stage, cp_stop):
    nc = bass.Bass("TRN2", target_bir_lowering=False)

    def din(name, shape, dt=F32):
        return nc.dram_tensor(name, shape, dt, kind="ExternalInput").ap()

    def dscr(name, shape, dt):
        return nc.dram_tensor(name, shape, dt, kind="Internal").ap()

    xc = din("xc", [NXT * 128, D])
    w_in_e = din("w_in_e", [D, 16384] if stage != 5 else [128, 128])
    w_out_e = din("w_out_e", [3072, D] if stage != 5 else [128, 128])
    w_in_o = din("w_in_o", [D, 12288] if stage >= 5 else [128, 128])
    w_out_o = din("w_out_o", [4096, D] if stage >= 5 else [128, 128])
    g_e = din("g_e", [D])
    g_o = din("g_o", [D])
    g_f = din("g_f", [D])
    b_o = din("b_o", [D])
    c1d = din("c1", [128, 32 * 37])
    ctabd = din("ctab", [128, NCT])
    tabBd = din("tabB", [128, NXT, 128])
    cbfd = din("cbf", [128, 3200], BF16)
    outd = nc.dram_tensor("out", [2048, D], F32, kind="ExternalOutput").ap()
    x1s = nc.dram_tensor("x1s", [TOK, D], F32, kind={1: "ExternalOutput", 5: "ExternalInput"}.get(stage, "Internal")).ap()
    khalo = dscr("khalo", [8, 128, 2688], BF16)
    vhalo = dscr("vhalo", [8, 128, 21, 128], BF16)
    yTs = dscr("yTs", [24, 128, TOK], BF16)
    vS = dscr("vS", [32, 128, 2048], F32)
    sgS = dscr("sgS", [32, 128, 2048], BF16)
    yT1s = dscr("yT1s", [32, 128, 2048], BF16)
    x2s = dscr("x2s", [2048, D], F32)

    with ExitStack() as st:
        S = Sched(nc, st)

        def sb(name, shape, dt):
            return st.enter_context(nc.sbuf_tensor(name, shape, dt))

        def ps(name, shape, dt):
            return st.enter_context(nc.psum_tensor(name, shape, dt))

        BIG = sb("BIG", [128, KC * TOK], BF16)
        hT = BIG[:, :].rearrange("p (k t) -> p k t", k=KC)
        wb = [sb("wb%d" % i, [128, 8192], BF16) for i in range(2)]
        xt = [sb("xt%d" % i, [128, D], F32) for i in range(2)]
        xn = sb("xn", [128, D], BF16)
        gb = sb("gb", [128, D], F32)
        ctab = sb("ctab_s", [128, NCT], F32)
        cbf = sb("cbf_s", [128, 3200], BF16)
        tB = [sb("tB%d" % i, [128, 128], F32) for i in range(2)]
        UNI = sb("UNI", [128, 27024], BF16)
        KT = UNI[:, 0:9216]
        Vb = UNI[:, 9216:18576].rearrange("p (s c) -> p s c", s=72)
        yTu = UNI[:, 18576:22928].rearrange("p (e t) -> p e t", e=2)
        Sf = UNI[:, 22928:27024].bitcast(F32).rearrange("p (h c) -> p h c", h=8)
        PT = [sb("PT%d" % i, [128, 512], BF16) for i in range(3)]
        small = sb("small", [128, 64], F32)
        tm = sb("tm", [128, 512], BF16)
        kb16 = sb("kb16", [128, 128], BF16)
        rp = sb("rp", [128, 5, 128], F32)
        QT = sb("QT", [128, 4, 128], BF16)
        sg = sb("sg", [128, 256], F32)
        ytm = sb("ytm", [128, 256], BF16)
        Sb16 = sb("Sb16", [128, 256], BF16)
        vb16 = sb("vb16", [128, 256], BF16)

        psP = [ps("psP%d" % i, [128, 512], F32) for i in range(2)]
        psS = [ps("psS%d" % i, [128, 512], F32) for i in range(2)]
        psO = [ps("psO%d" % i, [128, 512], F32) for i in range(2)]
        psT = [ps("psT%d" % i, [128, 8, 128], BF16) for i in range(2)]

        B = {}

        def b(name):
            if name not in B:
                B[name] = Buf(name)
            return B[name]

        identb = cbf[:, 0:128]
        identf = ctab[:, C_IDF:C_IDF + 128]
        TM = [b("tm0"), b("tm1"), b("tm2")]
        ALLPS = [b("psP0"), b("psP1"), b("psS0"), b("psS1"), b("psO0"), b("psO1")]
        cnt = {"P": 0, "S": 0, "O": 0, "T": 0, "W": 0, "X": 0, "PT": 0, "TB": 0}

        def nxt(k, n):
            cnt[k] += 1
            return cnt[k] % n

        S.dma("sp", ctab[:], ctabd[:, :], writes=[b("ctab")])
        S.dma("sp", cbf[:], cbfd[:, :], writes=[b("cbf")])
        S.op("dve", lambda e: e.memset(Vb[:, :, 128:130], 0.0), writes=[b("Vb")])
        for g in range(3):
            for s_ in range(HALO_T[g] + NT):
                t33 = 16 - HALO_T[g] + s_
                S.op("dve", lambda e, g=g, s_=s_, t33=t33: e.tensor_copy(
                    Vb[:, V_OFF[g] + s_, 128:129], ctab[:, C_VAL + t33:C_VAL + t33 + 1]),
                    reads=[b("ctab")], writes=[b("Vb")])

        def col(c0, n=1):
            return ctab[:, c0:c0 + n]

        def cp(n):
            if cp_stop is not None and n >= cp_stop:
                S.finish()
                raise StopBuild(nc)

        cp(1)

        def dbg(n):
            if stage == 0 and n >= cp_stop:
                S.finish()
                raise StopBuild(nc)
        def load_w(dst_view, src_ap, bw):
            S.dma("pool", dst_view, src_ap.rearrange("(k p) c -> p k c", p=128), writes=[bw])

        def build_hT(src, r0, ntiles, gvec):
            S.dma("sp", gb[:], gvec.partition_broadcast(128), writes=[b("gb")])
            for i in range(ntiles):
                xi = nxt("X", 2)
                S.dma("sp", xt[xi][:], src[r0 + i * 128:r0 + (i + 1) * 128, :], writes=[b("xt%d" % xi)])
                S.op("act", lambda e, xi=xi: e.activation(xn[:], xt[xi][:], AF.Square, accum_out=small[:, 0:1]),
                     reads=[b("xt%d" % xi)], writes=[b("xn"), b("ss")])
                S.op("dve", lambda e: e.tensor_scalar(small[:, 1:2], small[:, 0:1], 1.0 / D, EPS, ALU.mult, ALU.add),
                     reads=[b("ss")], writes=[b("ss1")])
                S.op("act", lambda e: e.activation(small[:, 2:3], small[:, 1:2], AF.Sqrt), reads=[b("ss1")], writes=[b("ss2")])
                S.op("dve", lambda e: e.reciprocal(small[:, 3:4], small[:, 2:3]), reads=[b("ss2")], writes=[b("rstd")])
                S.op("dve", lambda e, xi=xi: e.scalar_tensor_tensor(xn[:], xt[xi][:], small[:, 3:4], gb[:], ALU.mult, ALU.mult),
                     reads=[b("xt%d" % xi), b("rstd"), b("gb")], writes=[b("xn")])
                for half in range(2):
                    ti = nxt("T", 2)
                    for k in range(8):
                        kc = half * 8 + k
                        S.op("pe", lambda e, ti=ti, k=k, kc=kc: e.transpose(psT[ti][:, k, :], xn[:, kc * 128:(kc + 1) * 128], identb),
                             reads=[b("xn"), b("cbf")], writes=[b("psT%d" % ti)])
                    S.op("act", lambda e, ti=ti, half=half, i=i: e.copy(hT[:, half * 8:(half + 1) * 8, i * 128:(i + 1) * 128], psT[ti][:, :, :]),
                         reads=[b("psT%d" % ti)], writes=[b("hT%d" % i)])

        def proj_tm(i, wv, ncols, bw):
            pi = nxt("P", 2)
            for kc in range(KC):
                S.op("pe", lambda e, pi=pi, kc=kc, i=i: e.matmul(psP[pi][:, 0:ncols], hT[:, kc, i * 128:(i + 1) * 128], wv[:, kc, :],
                                                              start=(kc == 0), stop=(kc == KC - 1)),
                     reads=[b("hT%d" % i), bw], writes=[b("psP%d" % pi)])
            return pi

        def rope(pi, c0, nb, half, cos_ap, sin_ap, dst, bdst, rdeps):
            src = psP[pi][:, c0:c0 + nb * 128].rearrange("p (n c) -> p n c", n=nb)
            x1 = src[:, :, 0:half]
            x2 = src[:, :, half:2 * half]
            cb_ = cos_ap.unsqueeze(1).to_broadcast([128, nb, half])
            sb_ = sin_ap.unsqueeze(1).to_broadcast([128, nb, half])
            n = nb * half
            t = [rp[:, j, 0:n].rearrange("p (n c) -> p n c", n=nb) for j in range(4)]
            bp = b("psP%d" % pi)
            S.op("dve", lambda e: e.tensor_tensor(t[0], x1, cb_, ALU.mult), reads=[bp] + rdeps, writes=[b("rp0")])
            S.op("dve", lambda e: e.tensor_tensor(t[1], x2, sb_, ALU.mult), reads=[bp] + rdeps, writes=[b("rp1")])
            S.op("dve", lambda e: e.tensor_tensor(t[2], x2, cb_, ALU.mult), reads=[bp] + rdeps, writes=[b("rp2")])
            S.op("dve", lambda e: e.tensor_tensor(t[3], x1, sb_, ALU.mult), reads=[bp] + rdeps, writes=[b("rp3")])
            S.op("dve", lambda e: e.tensor_tensor(dst[:, :, 0:half], t[0], t[1], ALU.subtract),
                 reads=[b("rp0"), b("rp1")], writes=bdst)
            S.op("dve", lambda e: e.tensor_tensor(dst[:, :, half:2 * half], t[2], t[3], ALU.add),
                 reads=[b("rp2"), b("rp3")], writes=bdst)

        def load_tB(t65):
            ti = nxt("TB", 2)
            S.dma("sp", tB[ti][:], tabBd[:, t65, :], writes=[b("tB%d" % ti)])
            return ti

        def rope_b(pi, c0, t_i, dst, bdst, scale):
            src = psP[pi][:, c0:c0 + 128]
            x1, x2 = src[:, 0:64], src[:, 64:128]
            cs, sn = tB[t_i][:, 0:64], tB[t_i][:, 64:128]
            bp, bt = b("psP%d" % pi), b("tB%d" % t_i)
            o = rp[:, 4, 0:128]
            t = [rp[:, j, 0:64] for j in range(4)]
            S.op("dve", lambda e: e.tensor_tensor(t[0], x1, cs, ALU.mult), reads=[bp, bt], writes=[b("rp0")])
            S.op("dve", lambda e: e.tensor_tensor(t[1], x2, sn, ALU.mult), reads=[bp, bt], writes=[b("rp1")])
            S.op("dve", lambda e: e.tensor_tensor(t[2], x2, cs, ALU.mult), reads=[bp, bt], writes=[b("rp2")])
            S.op("dve", lambda e: e.tensor_tensor(t[3], x1, sn, ALU.mult), reads=[bp, bt], writes=[b("rp3")])
            S.op("dve", lambda e: e.tensor_tensor(o[:, 0:64], t[0], t[1], ALU.subtract), reads=[b("rp0"), b("rp1")], writes=[b("rp4")])
            S.op("dve", lambda e: e.tensor_tensor(o[:, 64:128], t[2], t[3], ALU.add), reads=[b("rp2"), b("rp3")], writes=[b("rp4")])
            if scale is None:
                S.op("act", lambda e: e.copy(dst, o), reads=[b("rp4")], writes=[bdst])
            else:
                S.op("act", lambda e: e.activation(dst, o, AF.Copy, scale=scale), reads=[b("rp4"), b("ctab")], writes=[bdst])

        def a_kv_part(h, tiles, t33_of, slot_of, kb):
            wv = [wb[kb][:, 0:KC * 384].rearrange("p (k c) -> p k c", k=KC),
                  wb[1 - kb][:, 0:KC * 384].rearrange("p (k c) -> p k c", k=KC)]
            bws = [b("wb%d" % kb), b("wb%d" % (1 - kb))]
            for part, c_base in ((0, AK0), (1, AV0)):
                for g in range(3):
                    load_w(wv[part][:, :, g * 128:(g + 1) * 128], w_in_e[:, c_base + g * 1024 + h * 128: c_base + g * 1024 + (h + 1) * 128], bws[part])
            for i in tiles:
                pi = proj_tm(i, wv[0], 384, bws[0])
                tmv = tm[:, 0:384].rearrange("p (n c) -> p n c", n=3)
                S.op("act", lambda e, pi=pi: e.copy(tm[:, 0:384], psP[pi][:, 0:384]), reads=[b("psP%d" % pi)], writes=TM)
                t33 = t33_of(i)
                rope(pi, 0, 3, 16, col(C_COSA + t33 * 16, 16), col(C_SINA + t33 * 16, 16), tmv, TM, [b("ctab")])
                ti = nxt("T", 2)
                for g in range(3):
                    S.op("pe", lambda e, ti=ti, g=g: e.transpose(psT[ti][:, g, :], tm[:, g * 128:(g + 1) * 128], identb),
                         reads=TM + [b("cbf")], writes=[b("psT%d" % ti)])
                for g in range(3):
                    sl = slot_of(g, i)
                    if sl is None:
                        continue
                    S.op("act", lambda e, ti=ti, g=g, sl=sl: e.copy(KT[:, KT_OFF[g] + sl * 128: KT_OFF[g] + (sl + 1) * 128], psT[ti][:, g, :]),
                         reads=[b("psT%d" % ti)], writes=[b("KT")])
            for i in tiles:
                pi = proj_tm(i, wv[1], 384, bws[1])
                for g in range(3):
                    sl = slot_of(g, i)
                    if sl is None:
                        continue
                    S.op("act", lambda e, pi=pi, g=g, sl=sl: e.copy(Vb[:, V_OFF[g] + sl, 0:128], psP[pi][:, g * 128:(g + 1) * 128]),
                         reads=[b("psP%d" % pi)], writes=[b("Vb")])

        if stage == 0:
            build_hT(xc, 0, 1, g_e)
            a_kv_part(0, range(1), lambda i: i, lambda g, i: 0, 0)
            dbg(0)
        def layer0():
            S.op("dve", lambda e: e.memset(Sf[:, :, :], 0.0), writes=[b("Sf")])
            for p in range(NSB):
                if stage == 2:
                    break
                S.barrier()
                build_hT(xc, p * NPT * 128, NPT, g_e)
                cp(2)
                for h in range(8):
                    wv = wb[h % 2][:, 0:KC * 384].rearrange("p (k c) -> p k c", k=KC)
                    bw = b("wb%d" % (h % 2))
                    load_w(wv[:, :, 0:128], w_in_e[:, BK0 + h * 128:BK0 + (h + 1) * 128], bw)
                    load_w(wv[:, :, 128:384], w_in_e[:, BV0 + h * 256:BV0 + (h + 1) * 256], bw)
                    for i in range(NPT):
                        t48 = p * NPT + i
                        pi = proj_tm(i, wv, 384, bw)
                        t_i = load_tB(t48)
                        rope_b(pi, 0, t_i, kb16[:], b("kb16"), col(C_KDP + t48 * 8 + h))
                        S.op("act", lambda e, pi=pi: e.copy(vb16[:], psP[pi][:, 128:384]), reads=[b("psP%d" % pi)], writes=[b("vb16")])
                        S.op("pe", lambda e, i=i: e.matmul(psO[0][:, 0:256], kb16[:], vb16[:], start=(i == 0), stop=(i == NPT - 1)),
                             reads=[b("kb16"), b("vb16")], writes=[b("psO0")])
                    S.op("dve", lambda e, h=h: e.tensor_tensor(Sf[:, h, :], psO[0][:, 0:256], Sf[:, h, :], ALU.add),
                         reads=[b("psO0"), b("Sf")], writes=[b("Sf")])
                    cp(3)
                cp(3.2 + 0.1 * p)
                if p == NSB - 1:
                    cp(3.5)
                    for h in range(8):
                        a_kv_part(h, range(NPT), lambda i: i,
                                  lambda g, i: (i - (NPT - HALO_T[g])) if i >= NPT - HALO_T[g] else None, h % 2)
                        cp(3.7)
                        for g in range(3):
                            hc = HALO_T[g] * 128
                            ko = (0, 128, 640)[g]
                            S.dma("sp", khalo[h, :, ko:ko + hc], KT[:, KT_OFF[g]:KT_OFF[g] + hc], reads=[b("KT")], writes=[b("khalo")], owner=b("KT"))
                            vo = (0, 1, 5)[g]
                            S.dma("sp", vhalo[h, :, vo:vo + HALO_T[g], :], Vb[:, V_OFF[g]:V_OFF[g] + HALO_T[g], 0:128],
                                  reads=[b("Vb")], writes=[b("vhalo")], owner=b("Vb"))
                        cp(3.8)

            cp(4)
            S.barrier()
            build_hT(xc, NPREV * 128, NT, g_e)
            cp(5)
            inv_sqrt = 128.0 ** -0.5
            for h in range(8):
                for g in range(3):
                    hc = HALO_T[g] * 128
                    ko = (0, 128, 640)[g]
                    vo = (0, 1, 5)[g]
                    S.dma("sp", KT[:, KT_OFF[g]:KT_OFF[g] + hc], khalo[h, :, ko:ko + hc], reads=[b("khalo")], writes=[b("KT")])
                    S.dma("sp", Vb[:, V_OFF[g]:V_OFF[g] + HALO_T[g], 0:128], vhalo[h, :, vo:vo + HALO_T[g], :], reads=[b("vhalo")], writes=[b("Vb")])
                a_kv_part(h, range(NT), lambda i: 16 + i, lambda g, i: HALO_T[g] + i, h % 2)
                wq = wb[h % 2][:, 0:KC * 512].rearrange("p (k c) -> p k c", k=KC)
                bwq = b("wb%d" % (h % 2))
                for g in range(3):
                    load_w(wq[:, :, g * 128:(g + 1) * 128], w_in_e[:, AQ0 + g * 1024 + h * 128:AQ0 + g * 1024 + (h + 1) * 128], bwq)
                load_w(wq[:, :, 384:512], w_in_e[:, G0 + h * 128:G0 + (h + 1) * 128], bwq)
                for i in range(NT):
                    pi = proj_tm(i, wq, 512, bwq)
                    tmv = tm[:, 0:384].rearrange("p (n c) -> p n c", n=3)
                    S.op("act", lambda e, pi=pi: e.copy(tm[:, 0:384], psP[pi][:, 0:384]), reads=[b("psP%d" % pi)], writes=TM)
                    S.op("act", lambda e, pi=pi: e.activation(sg[:, 0:128], psP[pi][:, 384:512], AF.Silu), reads=[b("psP%d" % pi)], writes=[b("sg")])
                    t33 = 16 + i
                    rope(pi, 0, 3, 16, col(C_COSA + t33 * 16, 16), col(C_SINA + t33 * 16, 16), tmv, TM, [b("ctab")])
                    ti = nxt("T", 2)
                    for g in range(3):
                        S.op("pe", lambda e, ti=ti, g=g: e.transpose(psT[ti][:, g, :], tm[:, g * 128:(g + 1) * 128], identb),
                             reads=TM + [b("cbf")], writes=[b("psT%d" % ti)])
                    S.op("act", lambda e, ti=ti: e.copy(QT[:, 0:3, :], psT[ti][:, 0:3, :]), reads=[b("psT%d" % ti)], writes=[b("QT")])
                    oi = nxt("O", 2)
                    for ch in range(6):
                        si = nxt("S", 2)
                        pti = nxt("PT", 3)
                        for j in range(4):
                            g, o = MI[ch * 4 + j]
                            sl = HALO_T[g] + i - o
                            S.op("pe", lambda e, si=si, j=j, g=g, sl=sl: e.matmul(
                                psS[si][:, j * 128:(j + 1) * 128], KT[:, KT_OFF[g] + sl * 128:KT_OFF[g] + (sl + 1) * 128], QT[:, g, :],
                                start=True, stop=True), reads=[b("KT"), b("QT")], writes=[b("psS%d" % si)])
                        S.op("act", lambda e, si=si, pti=pti: e.activation(PT[pti][:], psS[si][:], AF.Exp, scale=inv_sqrt),
                             reads=[b("psS%d" % si)], writes=[b("PT%d" % pti)])
                        S.op("dve", lambda e, pti=pti, ch=ch: e.tensor_tensor(PT[pti][:], PT[pti][:], cbf[:, 128 + ch * 512:128 + (ch + 1) * 512], ALU.mult),
                             reads=[b("PT%d" % pti), b("cbf")], writes=[b("PT%d" % pti)])
                        for j in range(4):
                            g, o = MI[ch * 4 + j]
                            sl = HALO_T[g] + i - o
                            S.op("pe", lambda e, oi=oi, pti=pti, j=j, g=g, sl=sl, ch=ch: e.matmul(
                                psO[oi][:, 0:129], PT[pti][:, j * 128:(j + 1) * 128], Vb[:, V_OFF[g] + sl, 0:129],
                                start=(ch == 0 and j == 0), stop=(ch == 5 and j == 3)),
                                reads=[b("PT%d" % pti), b("Vb")], writes=[b("psO%d" % oi)])
                    S.op("dve", lambda e, oi=oi: e.tensor_scalar_max(small[:, 8:9], psO[oi][:, 128:129], 1e-30),
                         reads=[b("psO%d" % oi)], writes=[b("den")])
                    S.op("dve", lambda e: e.reciprocal(small[:, 9:10], small[:, 8:9]), reads=[b("den")], writes=[b("rden")])
                    S.op("dve", lambda e, oi=oi: e.scalar_tensor_tensor(ytm[:, 0:128], psO[oi][:, 0:128], small[:, 9:10], sg[:, 0:128], ALU.mult, ALU.mult),
                         reads=[b("psO%d" % oi), b("rden"), b("sg")], writes=[b("ytm")])
                    ti = nxt("T", 2)
                    S.op("pe", lambda e, ti=ti: e.transpose(psT[ti][:, 0, :], ytm[:, 0:128], identb), reads=[b("ytm"), b("cbf")], writes=[b("psT%d" % ti)])
                    S.op("act", lambda e, ti=ti, i=i: e.copy(yTu[:, 0, i * 128:(i + 1) * 128], psT[ti][:, 0, :]), reads=[b("psT%d" % ti)], writes=[b("yTu")])
                S.dma("sp", yTs[h, :, :], yTu[:, 0, :], reads=[b("yTu")], writes=[b("yTs")], owner=b("yTu"))
                cp(6)
            cp(7)

            for h in range(8):
                if h % 2 == 0:
                    wq = wb[0][:, 0:KC * 512].rearrange("p (k c) -> p k c", k=KC)
                    wg = wb[1][:, 0:KC * 256].rearrange("p (k c) -> p k c", k=KC)
                    bwq, bwg = b("wb0"), b("wb1")
                else:
                    wq = UNI[:, 0:KC * 512].rearrange("p (k c) -> p k c", k=KC)
                    wg = UNI[:, 9216:9216 + KC * 256].rearrange("p (k c) -> p k c", k=KC)
                    bwq, bwg = b("KT"), b("Vb")
                load_w(wq[:, :, 0:128], w_in_e[:, BQ0 + h * 128:BQ0 + (h + 1) * 128], bwq)
                load_w(wq[:, :, 128:256], w_in_e[:, BK0 + h * 128:BK0 + (h + 1) * 128], bwq)
                load_w(wq[:, :, 256:512], w_in_e[:, BV0 + h * 256:BV0 + (h + 1) * 256], bwq)
                load_w(wg, w_in_e[:, G0 + 1024 + h * 256:G0 + 1024 + (h + 1) * 256], bwg)
                S.op("act", lambda e, h=h: e.copy(Sb16[:], Sf[:, h, :]), reads=[b("Sf")], writes=[b("Sb16")])
                cg = GAM[h] ** 128
                for i in range(NT):
                    pi = proj_tm(i, wq, 512, bwq)
                    t_i = load_tB(NPREV + i)
                    rope_b(pi, 0, t_i, tm[:, 0:128], b("tm0"), None)
                    rope_b(pi, 128, t_i, tm[:, 128:256], b("tm1"), col(C_KDO + h))
                    S.op("act", lambda e: e.copy(kb16[:], tm[:, 128:256]), reads=[b("tm1")], writes=[b("kb16")])
                    S.op("act", lambda e, pi=pi: e.copy(vb16[:], psP[pi][:, 256:512]), reads=[b("psP%d" % pi)], writes=[b("vb16")])
                    pg = proj_tm(i, wg, 256, bwg)
                    S.op("act", lambda e, pg=pg: e.activation(sg[:, 0:256], psP[pg][:, 0:256], AF.Silu), reads=[b("psP%d" % pg)], writes=[b("sg")])
                    ti = nxt("T", 2)
                    S.op("pe", lambda e, ti=ti: e.transpose(psT[ti][:, 0, :], tm[:, 0:128], identb), reads=[b("tm0"), b("cbf")], writes=[b("psT%d" % ti)])
                    S.op("pe", lambda e, ti=ti: e.transpose(psT[ti][:, 1, :], tm[:, 128:256], identb), reads=[b("tm1"), b("cbf")], writes=[b("psT%d" % ti)])
                    S.op("act", lambda e, ti=ti: e.copy(QT[:, 0:2, :], psT[ti][:, 0:2, :]), reads=[b("psT%d" % ti)], writes=[b("QT")])
                    S.op("pe", lambda e: e.matmul(psS[0][:, 0:128], QT[:, 1, :], QT[:, 0, :], start=True, stop=True),
                         reads=[b("QT")], writes=[b("psS0")])
                    S.op("dve", lambda e: e.tensor_tensor(PT[0][:, 0:128], psS[0][:, 0:128], cbf[:, 128:256], ALU.mult),
                         reads=[b("psS0"), b("cbf")], writes=[b("PT0")])
                    oi = nxt("O", 2)
                    S.op("pe", lambda e, oi=oi: e.matmul(psO[oi][:, 0:256], PT[0][:, 0:128], vb16[:], start=True, stop=False),
                         reads=[b("PT0"), b("vb16")], writes=[b("psO%d" % oi)])
                    S.op("pe", lambda e, oi=oi: e.matmul(psO[oi][:, 0:256], QT[:, 0, :], Sb16[:], start=False, stop=True),
                         reads=[b("QT"), b("Sb16")], writes=[b("psO%d" % oi)])
                    S.op("pe", lambda e: e.matmul(psS[1][:, 0:256], kb16[:], vb16[:], start=True, stop=True),
                         reads=[b("kb16"), b("vb16")], writes=[b("psS1")])
                    S.op("act", lambda e, oi=oi: e.activation(xn[:, 0:256], psO[oi][:, 0:256], AF.Square, accum_out=small[:, 12:13]),
                         reads=[b("psO%d" % oi)], writes=[b("xn"), b("zss")])
                    S.op("dve", lambda e, h=h: e.tensor_scalar(small[:, 13:14], small[:, 12:13], 1.0 / 256, col(C_EPSB + h), ALU.mult, ALU.add),
                         reads=[b("zss"), b("ctab")], writes=[b("zs1")])
                    S.op("act", lambda e: e.activation(small[:, 14:15], small[:, 13:14], AF.Sqrt), reads=[b("zs1")], writes=[b("zs2")])
                    S.op("dve", lambda e: e.reciprocal(small[:, 15:16], small[:, 14:15]), reads=[b("zs2")], writes=[b("zr")])
                    S.op("dve", lambda e, oi=oi: e.scalar_tensor_tensor(ytm[:, 0:256], psO[oi][:, 0:256], small[:, 15:16], sg[:, 0:256], ALU.mult, ALU.mult),
                         reads=[b("psO%d" % oi), b("zr"), b("sg")], writes=[b("ytm")])
                    ti = nxt("T", 2)
                    for e2 in range(2):
                        S.op("pe", lambda e, ti=ti, e2=e2: e.transpose(psT[ti][:, e2, :], ytm[:, e2 * 128:(e2 + 1) * 128], identb),
                             reads=[b("ytm"), b("cbf")], writes=[b("psT%d" % ti)])
                    S.op("act", lambda e, ti=ti, i=i: e.copy(yTu[:, :, i * 128:(i + 1) * 128], psT[ti][:, 0:2, :]), reads=[b("psT%d" % ti)], writes=[b("yTu")])
                    S.op("dve", lambda e, h=h: e.tensor_tensor(Sf[:, h, :], psS[1][:, 0:256], Sf[:, h, :], ALU.add),
                         reads=[b("psS1"), b("Sf")], writes=[b("Sf")])
                    S.op("dve", lambda e, h=h, cg=cg: e.tensor_scalar_mul(Sf[:, h, :], Sf[:, h, :], cg), reads=[b("Sf")], writes=[b("Sf")])
                    S.op("act", lambda e, h=h: e.copy(Sb16[:], Sf[:, h, :]), reads=[b("Sf")], writes=[b("Sb16")])
                for e2 in range(2):
                    S.dma("sp", yTs[8 + 2 * h + e2, :, :], yTu[:, e2, :], reads=[b("yTu")], writes=[b("yTs")], owner=b("yTu"))

            cp(8)
            S.barrier()
            for (t0, t1) in ((0, 9), (9, 17)):
                S.barrier()
                ntk = (t1 - t0) * 128
                yv = BIG[:, 0:24 * ntk].rearrange("p (k t) -> p k t", k=24)
                for k in range(24):
                    S.dma("sp", yv[:, k, :], yTs[k, :, t0 * 128:t1 * 128], reads=[b("yTs")], writes=[b("yv")])
                for cbk in range(8):
                    wi = nxt("W", 2)
                    wv = wb[wi][:, 0:24 * 256].rearrange("p (k c) -> p k c", k=24)
                    load_w(wv, w_out_e[:, cbk * 256:(cbk + 1) * 256], b("wb%d" % wi))
                    for i in range(t0, t1):
                        xi = nxt("X", 2)
                        S.dma("sp", xt[xi][:, 0:256], xc[(NPREV + i) * 128:(NPREV + i + 1) * 128, cbk * 256:(cbk + 1) * 256], writes=[b("xt%d" % xi)])
                        pi = nxt("P", 2)
                        for k in range(24):
                            S.op("pe", lambda e, pi=pi, k=k, i=i, wv=wv, yv=yv, t0=t0: e.matmul(psP[pi][:, 0:256], yv[:, k, (i - t0) * 128:(i - t0 + 1) * 128], wv[:, k, :],
                                                                          start=(k == 0), stop=(k == 23)),
                                 reads=[b("yv"), b("wb%d" % wi)], writes=[b("psP%d" % pi)])
                        S.op("dve", lambda e, pi=pi, xi=xi: e.tensor_tensor(xt[xi][:, 256:512], psP[pi][:, 0:256], xt[xi][:, 0:256], ALU.add),
                             reads=[b("psP%d" % pi), b("xt%d" % xi)], writes=[b("xo%d" % xi)])
                        S.dma("sp", x1s[i * 128:(i + 1) * 128, cbk * 256:(cbk + 1) * 256], xt[xi][:, 256:512],
                              reads=[b("xo%d" % xi)], writes=[b("x1s")], owner=b("xo%d" % xi))
            S.barrier()
        if stage != 5:
            layer0()
        if stage < 5:
            S.finish()
            return nc

        ubuf = UNI[:, 0:2176]
        dg = UNI[:, 2176:6144].rearrange("p (k c) -> p k c", k=31)
        sgbuf = UNI[:, 6144:8192]
        vbuf = UNI[:, 8192:12288].bitcast(F32)
        sig = UNI[:, 12288:13312].bitcast(F32)
        Abuf = UNI[:, 13312:17408].bitcast(F32)
        Bbuf = UNI[:, 17408:21504].bitcast(F32)
        vb2 = UNI[:, 21504:22016]
        vsq = UNI[:, 22016:22528]
        c1 = UNI[:, 22528:24896].bitcast(F32)
        onesb = UNI[:, 24896:24898]
        ybuf = UNI[:, 24960:27008]
        onesf = ctab[:, C_ONE:C_ONE + 128]
        S.barrier()
        S.dma("sp", c1, c1d[:, :], writes=[b("c1")])
        S.op("dve", lambda e: e.memset(onesb, 1.0), writes=[b("onesb")])
        S.op("dve", lambda e: e.memset(PT[0][:, :], 0.0), writes=[b("PT0")])
        build_hT(x1s, 0, NT, g_o)
        S.op("pe", lambda e: e.matmul(psO[0][:, 0:32], PT[0][:, 0:128], PT[0][:, 0:32], start=True, stop=False, skip_group_check=True),
             reads=[b("PT0")], writes=[b("psO0")])

        def c1c(c):
            return c1[:, c:c + 1]

        groups = [(0, 128)] + [(128 + m * 512, 512) for m in range(4)]
        for j in range(32):
            wi = nxt("W", 2)
            bw = b("wb%d" % wi)
            wv = wb[wi][:, 0:KC * 384].rearrange("p (k c) -> p k c", k=KC)
            for part in range(3):
                load_w(wv[:, :, part * 128:(part + 1) * 128], w_in_o[:, part * 4096 + j * 128:part * 4096 + (j + 1) * 128], bw)
            for k in range(31):
                S.op("dve", lambda e, k=k, j=j: e.tensor_scalar_mul(dg[:, k, :], identb, c1c(192 + j * 31 + k)),
                     reads=[b("cbf"), b("c1")], writes=[b("dg")])
            for gi, (t0, n) in enumerate(groups):
                tl = list(range(t0 // 128, (t0 + n) // 128))
                parts = ((0, psP[0], "psP0"), (1, psP[1], "psP1")) + (((2, psS[0], "psS0"),) if gi > 0 else ())
                for part, pdst, pn in parts:
                    for kc in range(KC):
                        S.op("pe", lambda e, pdst=pdst, part=part, kc=kc, t0=t0, n=n, wv=wv: e.matmul(
                            pdst[:, 0:n], wv[:, kc, part * 128:(part + 1) * 128], hT[:, kc, t0:t0 + n], start=(kc == 0), stop=(kc == KC - 1)),
                            reads=[b("hT%d" % t) for t in tl] + [bw], writes=[b(pn)])
                S.op("act", lambda e, n=n, j=j: e.activation(sig[:, 0:n], psP[1][:, 0:n], AF.Sigmoid, bias=c1c(32 + j)),
                     reads=[b("psP1"), b("c1")], writes=[b("sig")])
                S.op("dve", lambda e, n=n, j=j, t0=t0: e.scalar_tensor_tensor(ubuf[:, t0:t0 + n], psP[0][:, 0:n], c1c(j), sig[:, 0:n], ALU.add, ALU.mult),
                     reads=[b("psP0"), b("sig"), b("c1")], writes=[b("ubuf")])
                if gi == 0:
                    S.op("dve", lambda e: e.tensor_scalar_mul(ubuf[:, 0:128], ubuf[:, 0:128], col(C_UV)),
                         reads=[b("ubuf"), b("ctab")], writes=[b("ubuf")])
                else:
                    S.op("act", lambda e, n=n, j=j, t0=t0: e.activation(sgbuf[:, t0 - 128:t0 - 128 + n], psS[0][:, 0:n], AF.Silu, bias=c1c(64 + j)),
                         reads=[b("psS0"), b("c1")], writes=[b("sgbuf")])
            for m in range(4):
                for k in range(31):
                    S.op("pe", lambda e, k=k, m=m: e.matmul(psS[1][:, 0:512], dg[:, k, :], ubuf[:, 98 + m * 512 + k:98 + m * 512 + k + 512],
                                                       start=(k == 0), stop=(k == 30)),
                         reads=[b("dg"), b("ubuf")], writes=[b("psS1")])
                S.op("act", lambda e, m=m, j=j: e.activation(vbuf[:, m * 512:(m + 1) * 512], psS[1][:, 0:512], AF.Identity, bias=c1c(96 + j)),
                     reads=[b("psS1"), b("c1")], writes=[b("vbuf")])
                S.op("act", lambda e, m=m: e.copy(vb2, vbuf[:, m * 512:(m + 1) * 512]), reads=[b("vbuf")], writes=[b("vb2")])
                S.op("act", lambda e, m=m: e.activation(vsq, vbuf[:, m * 512:(m + 1) * 512], AF.Square), reads=[b("vbuf")], writes=[b("vsq")])
                for t in range(4):
                    tile_ = m * 4 + t
                    last = (j == 31)
                    S.op("pe", lambda e, t=t, tile_=tile_, last=last: e.matmul(psO[0][:, 2 * tile_:2 * tile_ + 1], vb2[:, t * 128:(t + 1) * 128], onesb[:, 0:1],
                                                                       start=False, stop=last, skip_group_check=True),
                         reads=[b("vb2"), b("onesb")], writes=[b("psO0")])
                    S.op("pe", lambda e, t=t, tile_=tile_, last=last: e.matmul(psO[0][:, 2 * tile_ + 1:2 * tile_ + 2], vsq[:, t * 128:(t + 1) * 128], onesb[:, 0:1],
                                                                       start=False, stop=last, skip_group_check=True),
                         reads=[b("vsq"), b("onesb")], writes=[b("psO0")])
            S.dma("sp", vS[j, :, :], vbuf, reads=[b("vbuf")], writes=[b("vS")], owner=b("vbuf"))
            S.dma("sp", sgS[j, :, :], sgbuf, reads=[b("sgbuf")], writes=[b("sgS")], owner=b("sgbuf"))

        st1 = rp[:, 2, :]
        S.op("dve", lambda e: e.tensor_scalar_mul(st1[:, 0:32], psO[0][:, 0:32], 1.0 / 4096), reads=[b("psO0")], writes=[b("st1")])
        mean = st1[:, 0:32].rearrange("p (t s) -> p t s", s=2)[:, :, 0:1]
        ex2 = st1[:, 0:32].rearrange("p (t s) -> p t s", s=2)[:, :, 1:2]

        def sv(c0):
            return st1[:, c0:c0 + 16].unsqueeze(2)
        S.op("dve", lambda e: e.tensor_tensor(sv(32), mean, mean, ALU.mult), reads=[b("st1")], writes=[b("st1")])
        S.op("dve", lambda e: e.tensor_tensor(sv(48), ex2, sv(32), ALU.subtract), reads=[b("st1")], writes=[b("st1")])
        S.op("dve", lambda e: e.tensor_scalar_add(st1[:, 48:64], st1[:, 48:64], EPS), reads=[b("st1")], writes=[b("st1")])
        S.op("act", lambda e: e.activation(st1[:, 48:64], st1[:, 48:64], AF.Sqrt), reads=[b("st1")], writes=[b("st1")])
        S.op("dve", lambda e: e.reciprocal(st1[:, 64:80], st1[:, 48:64]), reads=[b("st1")], writes=[b("st1")])
        S.op("dve", lambda e: e.scalar_tensor_tensor(sv(80), mean, -1.0, sv(64), ALU.mult, ALU.mult), reads=[b("st1")], writes=[b("st1")])
        for t in range(16):
            S.op("dve", lambda e, t=t: e.tensor_scalar_mul(rp[:, 0, :], identf, st1[:, 64 + t:65 + t]), reads=[b("st1"), b("ctab")], writes=[b("rp0")])
            S.op("dve", lambda e, t=t: e.tensor_scalar_mul(rp[:, 1, :], identf, st1[:, 80 + t:81 + t]), reads=[b("st1"), b("ctab")], writes=[b("rp1")])
            S.op("pe", lambda e: e.matmul(psS[0][:, 0:128], onesf, rp[:, 0, :], start=True, stop=True), reads=[b("ctab"), b("rp0")], writes=[b("psS0")])
            S.op("pe", lambda e: e.matmul(psS[0][:, 128:256], onesf, rp[:, 1, :], start=True, stop=True), reads=[b("ctab"), b("rp1")], writes=[b("psS0")])
            S.op("act", lambda e, t=t: e.copy(Abuf[:, t * 128:(t + 1) * 128], psS[0][:, 0:128]), reads=[b("psS0")], writes=[b("Abuf")])
            S.op("act", lambda e, t=t: e.copy(Bbuf[:, t * 128:(t + 1) * 128], psS[0][:, 128:256]), reads=[b("psS0")], writes=[b("Bbuf")])
        for j in range(32):
            S.dma("sp", vbuf, vS[j, :, :], reads=[b("vS")], writes=[b("vbuf")])
            S.dma("sp", sgbuf, sgS[j, :, :], reads=[b("sgS")], writes=[b("sgbuf")])
            S.op("dve", lambda e: e.tensor_tensor(vbuf, vbuf, Abuf, ALU.mult), reads=[b("vbuf"), b("Abuf")], writes=[b("vbuf")])
            S.op("pool", lambda e: e.tensor_tensor(vbuf, vbuf, Bbuf, ALU.add), reads=[b("vbuf"), b("Bbuf")], writes=[b("vbuf")])
            S.op("act", lambda e, j=j: e.activation(vbuf, vbuf, AF.Silu, bias=c1c(160 + j), scale=c1c(128 + j)),
                 reads=[b("vbuf"), b("c1")], writes=[b("vbuf")])
            S.op("dve", lambda e: e.tensor_tensor(ybuf, vbuf, sgbuf, ALU.mult), reads=[b("vbuf"), b("sgbuf")], writes=[b("ybuf")])
            S.dma("sp", yT1s[j, :, :], ybuf, reads=[b("ybuf")], writes=[b("yT1s")], owner=b("ybuf"))

        S.barrier()
        S.dma("sp", gb[:], b_o.partition_broadcast(128), writes=[b("gb")])
        for tg in range(2):
            S.barrier()
            yv = BIG[:, 0:32 * 1024].rearrange("p (k t) -> p k t", k=32)
            for k in range(32):
                S.dma("sp", yv[:, k, :], yT1s[k, :, tg * 1024:(tg + 1) * 1024], reads=[b("yT1s")], writes=[b("yv")])
            for cbk in range(8):
                wi = nxt("W", 2)
                wv = wb[wi][:, 0:32 * 256].rearrange("p (k c) -> p k c", k=32)
                load_w(wv, w_out_o[:, cbk * 256:(cbk + 1) * 256], b("wb%d" % wi))
                for i8 in range(8):
                    tile_ = tg * 8 + i8
                    xi = nxt("X", 2)
                    S.dma("sp", xt[xi][:, 0:256], x1s[(tile_ + 1) * 128:(tile_ + 2) * 128, cbk * 256:(cbk + 1) * 256],
                          reads=[b("x1s")], writes=[b("xt%d" % xi)])
                    pi = nxt("P", 2)
                    for k in range(32):
                        S.op("pe", lambda e, pi=pi, k=k, i8=i8, wv=wv, yv=yv: e.matmul(psP[pi][:, 0:256], yv[:, k, i8 * 128:(i8 + 1) * 128], wv[:, k, :],
                                                                           start=(k == 0), stop=(k == 31)),
                             reads=[b("yv"), b("wb%d" % wi)], writes=[b("psP%d" % pi)])
                    S.op("dve", lambda e, pi=pi, xi=xi: e.tensor_tensor(xt[xi][:, 256:512], psP[pi][:, 0:256], xt[xi][:, 0:256], ALU.add),
                         reads=[b("psP%d" % pi), b("xt%d" % xi)], writes=[b("xo%d" % xi)])
                    S.op("dve", lambda e, xi=xi, cbk=cbk: e.tensor_tensor(xt[xi][:, 256:512], xt[xi][:, 256:512], gb[:, cbk * 256:(cbk + 1) * 256], ALU.add),
                         reads=[b("xo%d" % xi), b("gb")], writes=[b("xo%d" % xi)])
                    S.dma("sp", x2s[tile_ * 128:(tile_ + 1) * 128, cbk * 256:(cbk + 1) * 256], xt[xi][:, 256:512],
                          reads=[b("xo%d" % xi)], writes=[b("x2s")], owner=b("xo%d" % xi))

        S.barrier()
        S.dma("sp", gb[:], g_f.partition_broadcast(128), writes=[b("gb")])
        for tile_ in range(16):
            xi = nxt("X", 2)
            S.dma("sp", xt[xi][:], x2s[tile_ * 128:(tile_ + 1) * 128, :], reads=[b("x2s")], writes=[b("xt%d" % xi), b("xo%d" % xi)])
            S.op("act", lambda e, xi=xi: e.activation(xn[:], xt[xi][:], AF.Square, accum_out=small[:, 0:1]),
                 reads=[b("xt%d" % xi)], writes=[b("xn"), b("ss")])
            S.op("dve", lambda e: e.tensor_scalar(small[:, 1:2], small[:, 0:1], 1.0 / D, EPS, ALU.mult, ALU.add), reads=[b("ss")], writes=[b("ss1")])
            S.op("act", lambda e: e.activation(small[:, 2:3], small[:, 1:2], AF.Sqrt), reads=[b("ss1")], writes=[b("ss2")])
            S.op("dve", lambda e: e.reciprocal(small[:, 3:4], small[:, 2:3]), reads=[b("ss2")], writes=[b("rstd")])
            S.op("dve", lambda e, xi=xi: e.scalar_tensor_tensor(xt[xi][:], xt[xi][:], small[:, 3:4], gb[:], ALU.mult, ALU.mult),
                 reads=[b("xt%d" % xi), b("rstd"), b("gb")], writes=[b("xt%d" % xi)])
            S.dma("sp", outd[tile_ * 128:(tile_ + 1) * 128, :], xt[xi][:], reads=[b("xt%d" % xi)], writes=[b("out")], owner=b("xt%d" % xi))
        S.finish()
    return nc


def _f32(a):
    return np.ascontiguousarray(a, dtype=np.float32)


def make_consts(q):
    T0 = 2048 * q
    p = np.arange(128)
    ct = np.zeros((128, NCT), np.float32)
    invA = (np.float32(500000.0) ** (-np.arange(0, 32, 2, dtype=np.float32) / np.float32(32))).astype(np.float32)
    for t in range(33):
        pos = (T0 - 128 - 2048 + t * 128 + p).astype(np.float32)
        ang = pos[:, None] * invA[None, :]
        ct[:, C_COSA + t * 16:C_COSA + (t + 1) * 16] = np.cos(ang)
        ct[:, C_SINA + t * 16:C_SINA + (t + 1) * 16] = np.sin(ang)
        ct[:, C_VAL + t] = 1.0 if (T0 - 128 - 2048 + t * 128) >= 0 else 0.0
    logg = np.log1p(-np.power(2.0, -5.0 - np.arange(8, dtype=np.float64)))
    for t in range(NPREV):
        dist = (NPREV * 128 - 1) - (t * 128 + p)
        ct[:, C_KDP + t * 8:C_KDP + (t + 1) * 8] = np.exp(dist[:, None] * logg[None, :]) * (128.0 ** -0.5)
    ct[:, C_KDO:C_KDO + 8] = np.exp(-(p[:, None] + 1.0) * logg[None, :]) * (128.0 ** -0.5)
    ct[:, C_EPSB:C_EPSB + 8] = EPS * np.exp(-2.0 * (p[:, None] + 1.0) * logg[None, :])
    ct[:, C_UV] = 1.0 if q > 0 else 0.0
    ct[:, C_IDF:C_IDF + 128] = np.eye(128, dtype=np.float32)
    ct[:, C_ONE:C_ONE + 128] = 1.0
    invB = (np.float32(10000.0) ** (-np.arange(0, 128, 2, dtype=np.float32) / np.float32(128))).astype(np.float32)
    tb = np.zeros((128, NXT, 128), np.float32)
    for t in range(NXT):
        pos = (T0 - 128 - NPREV * 128 + t * 128 + p).astype(np.float32)
        ang = pos[:, None] * invB[None, :]
        tb[:, t, 0:64] = np.cos(ang)
        tb[:, t, 64:128] = np.sin(ang)
    cb = np.zeros((128, 3200), np.float32)
    cb[:, 0:128] = np.eye(128)
    kk = p[:, None]
    qq = p[None, :]
    for m, (g, o) in enumerate(MI):
        d = DIL[g]
        delta = 128 * o + qq - kk
        cb[:, 128 + m * 128:128 + (m + 1) * 128] = ((delta >= 0) & (delta % d == 0) & (delta <= 128 * d)).astype(np.float32)
    return ct, tb, cb.astype(ml_dtypes.bfloat16)


_NC_CACHE = {}


def make_in_maps(inputs):
    x = _f32(inputs["x"])
    shared = {
        "w_in_e": _f32(inputs["w_in_even"][0]), "w_out_e": _f32(inputs["w_out_even"][0]),
        "w_in_o": _f32(inputs["w_in_odd"][0]), "w_out_o": _f32(inputs["w_out_odd"][0]),
        "g_e": _f32(inputs["norm_even"][0]), "g_o": _f32(inputs["norm_odd"][0]),
        "g_f": _f32(inputs["final_norm"]), "b_o": _f32(inputs["b_out_odd"][0]),
    }
    def pj(v):
        return _f32(v).reshape(32, 128).T
    b_in = _f32(inputs["b_in_odd"][0])
    cw = _f32(inputs["conv_w_odd"][0])
    c1 = np.zeros((128, 32 * 37), np.float32)
    for idx, v in enumerate((b_in[0:4096], b_in[4096:8192], b_in[8192:12288], inputs["conv_b_odd"][0],
                             inputs["ln_g_odd"][0], inputs["ln_b_odd"][0])):
        c1[:, idx * 32:(idx + 1) * 32] = pj(v)
    c1[:, 192:] = cw.reshape(31, 32, 128).transpose(2, 1, 0).reshape(128, 32 * 31)
    shared["c1"] = c1
    maps = []
    for c in range(8):
        bi, q = c // 4, c % 4
        T0 = 2048 * q
        xcore = np.zeros((NXT * 128, D), np.float32)
        lo = T0 - 128 - NPREV * 128
        s0 = max(lo, 0)
        xcore[s0 - lo:] = x[bi, s0:T0 + 2048]
        ct, tb, cb = make_consts(q)
        m = dict(shared)
        m.update({"xc": xcore, "ctab": ct, "tabB": tb, "cbf": cb})
        maps.append(m)
    return maps


def kernel(**inputs):
    if "nc" not in _NC_CACHE:
        _NC_CACHE["nc"] = build(9)
    nc = _NC_CACHE["nc"]
    maps = make_in_maps(inputs)
    res = run_bass_kernel_spmd(nc, maps, core_ids=list(range(8)))
    out = np.zeros((2, 8192, D), np.float32)
    for c in range(8):
        bi, q = c // 4, c % 4
        out[bi, q * 2048:(q + 1) * 2048] = res.results[c]["out"]
    return out
```
